# Optimizing a Trainium2 kernel written in Bass

```python
import math
import jax
import jax.numpy as jnp
from jax import lax
import numpy as np

D_MODEL = 2048
BATCH = 8
SEQ = 2048
DEPTH = 2

HEAD_DIM = 128
Q_BLOCK = 128
A_GROUPS = ((128, 1), (512, 4), (2048, 16))
A_HEADS_PER_GROUP = 2
A_HEADS = A_HEADS_PER_GROUP * len(A_GROUPS)
A_OUT = A_HEADS_PER_GROUP * HEAD_DIM
B_HEADS = 4
CMP_LEN = 32
CMP_STRIDE = 16
SLC_LEN = 64
N_SELECT = 16
WIN = 512
B_OUT = B_HEADS * HEAD_DIM
C_HEADS = 6
C_OUT = C_HEADS * HEAD_DIM
N_BUCKETS = 32
MAX_DISTANCE = 2048
BIAS_HEADS = A_HEADS + B_HEADS
D_FF = 5632
CONV_W = 3
ALPHA = (2 * DEPTH) ** 0.25
BETA = (8 * DEPTH) ** -0.25
LN_EPS = 1e-5
NEG_INF = -1e30
FORCE_SCORE = 1e9
ATTN_SCALE = HEAD_DIM ** -0.5

IN_SPLITS = (
    ('a_q', A_HEADS * HEAD_DIM), ('a_k', A_HEADS * HEAD_DIM), ('a_v', A_HEADS * HEAD_DIM),
    ('b_q', B_HEADS * HEAD_DIM),
    ('b_k_cmp', HEAD_DIM), ('b_v_cmp', HEAD_DIM),
    ('b_k_slc', HEAD_DIM), ('b_v_slc', HEAD_DIM),
    ('b_k_win', HEAD_DIM), ('b_v_win', HEAD_DIM),
    ('b_gate', 3 * B_HEADS),
    ('c_q', C_HEADS * HEAD_DIM), ('c_k', C_HEADS * HEAD_DIM), ('c_v', C_HEADS * HEAD_DIM),
    ('c_f', C_HEADS),
)
V_COLUMNS = ('a_v', 'b_v_cmp', 'b_v_slc', 'b_v_win', 'c_v')
N_IN = sum(w for _, w in IN_SPLITS)

kernel_name = 'hybrid_gated_dilated_nsa_fox_block'


def t5_bucket(dist):
    max_exact = N_BUCKETS // 2
    d = jnp.maximum(dist, 0)
    log_ratio = jnp.log(jnp.maximum(d, 1).astype(jnp.float32) / max_exact) / math.log(MAX_DISTANCE / max_exact)
    large = jnp.minimum(max_exact + (log_ratio * (N_BUCKETS - max_exact)).astype(jnp.int32), N_BUCKETS - 1)
    return jnp.where(d < max_exact, d, large)


def rel_bias_lookup(table, dist):
    return table[t5_bucket(dist)].astype(jnp.float32)


def layer_norm(x, g, b):
    xf = x.astype(jnp.float32)
    mu = jnp.mean(xf, axis=-1, keepdims=True)
    var = jnp.mean(jnp.square(xf - mu), axis=-1, keepdims=True)
    return ((xf - mu) * lax.rsqrt(var + LN_EPS) * g + b).astype(x.dtype)


def split_columns(u):
    offs = np.cumsum([w for _, w in IN_SPLITS])[:-1].tolist()
    return dict(zip([n for n, _ in IN_SPLITS], jnp.split(u, offs, axis=-1)))


def unblock(o):
    o = jnp.moveaxis(o, 0, 1)
    return o.reshape(o.shape[0], o.shape[1] * o.shape[2], *o.shape[3:])


def dilated_mixer(q, k, v, bias_tab):
    bsz, seq = q.shape[0], q.shape[1]
    n_blk = seq // Q_BLOCK
    outs, lses = [], []
    for g, (window, dil) in enumerate(A_GROUPS):
        hs = slice(g * A_HEADS_PER_GROUP, (g + 1) * A_HEADS_PER_GROUP)
        qg, kg, vg = q[:, :, hs], k[:, :, hs], v[:, :, hs]
        dist = jnp.arange(window // dil + 1) * dil
        bias = jnp.moveaxis(rel_bias_lookup(bias_tab[:, hs], dist), -1, 0)

        def block(i):
            t = i * Q_BLOCK + jnp.arange(Q_BLOCK)
            key_pos = t[:, None] - dist[None, :]
            idx = jnp.maximum(key_pos, 0)
            qb = lax.dynamic_slice_in_dim(qg, i * Q_BLOCK, Q_BLOCK, axis=1)
            kb, vb = kg[:, idx], vg[:, idx]
            s = jnp.einsum('bqhd,bqkhd->bhqk', qb, kb).astype(jnp.float32) * ATTN_SCALE + bias[None, :, None, :]
            s = jnp.where((key_pos >= 0)[None, None], s, NEG_INF)
            lse = jax.nn.logsumexp(s, axis=-1)
            p = jnp.exp(s - lse[..., None]).astype(vb.dtype)
            o = jnp.einsum('bhqk,bqkhd->bqhd', p, vb)
            return o, jnp.transpose(lse, (0, 2, 1))

        o_g, lse_g = lax.map(block, jnp.arange(n_blk))
        outs.append(unblock(o_g))
        lses.append(unblock(lse_g))
    w = jax.nn.softmax(jnp.stack(lses), axis=0)
    o = jnp.einsum('gbsh,gbshd->bshd', w.astype(v.dtype), jnp.stack(outs))
    return o.reshape(bsz, seq, A_OUT)


def nsa_mixer(q, k_cmp_src, v_cmp_src, k_slc, v_slc, k_win, v_win, gate_logits, b_gate,
              cmp_pe, cmp_w1, cmp_b1, cmp_w2, cmp_b2, bias_tab):
    bsz, seq = q.shape[0], q.shape[1]
    n_blk = seq // Q_BLOCK
    n_cmp = (seq - CMP_LEN) // CMP_STRIDE + 1
    n_slc = seq // SLC_LEN
    k_sel = min(N_SELECT, n_slc)
    pos = jnp.arange(seq)
    cidx = jnp.arange(n_cmp)[:, None] * CMP_STRIDE + jnp.arange(CMP_LEN)[None, :]
    kv = jnp.stack([k_cmp_src, v_cmp_src], axis=1)[:, :, cidx] + cmp_pe[None, :, None]
    kv = kv.reshape(bsz, 2, n_cmp, CMP_LEN * HEAD_DIM)
    hid = jax.nn.gelu(jnp.einsum('bcnf,cfe->bcne', kv, cmp_w1) + cmp_b1[None, :, None], approximate=False)
    kv_c = jnp.einsum('bcne,ced->bcnd', hid, cmp_w2) + cmp_b2[None, :, None]
    k_c, v_c = kv_c[:, 0], kv_c[:, 1]
    dist_c = pos[:, None] - (jnp.arange(n_cmp) * CMP_STRIDE + CMP_LEN - 1)[None, :]
    valid_c = (dist_c >= 0)[None, None]
    bias_c = jnp.moveaxis(rel_bias_lookup(bias_tab, dist_c), -1, 0)[None]
    s_c = jnp.einsum('bshd,bnd->bhsn', q, k_c).astype(jnp.float32) * ATTN_SCALE + bias_c
    p_c = jnp.where(valid_c, jax.nn.softmax(jnp.where(valid_c, s_c, NEG_INF), axis=-1), 0.0)
    o_cmp = jnp.einsum('bhsn,bnd->bshd', p_c.astype(v_c.dtype), v_c)
    ci = np.arange(n_cmp)[:, None]
    sj = np.arange(n_slc)[None, :]
    overlap = ((ci * CMP_STRIDE <= sj * SLC_LEN + SLC_LEN - 1)
               & (ci * CMP_STRIDE + CMP_LEN - 1 >= sj * SLC_LEN)).astype(np.float32)
    imp = jnp.einsum('bhsn,nj->bsj', p_c, jnp.asarray(overlap))
    blk = jnp.arange(n_slc)[None, :]
    cur = (pos // SLC_LEN)[:, None]
    forced = (blk == 0) | (blk == cur) | (blk == cur - 1)
    causal = blk * SLC_LEN <= pos[:, None]
    score = jnp.where(forced[None], FORCE_SCORE, jnp.where(causal[None], imp, NEG_INF))
    _, sel = lax.top_k(score, k_sel)
    k_blocks = k_slc.reshape(bsz, n_slc, SLC_LEN, HEAD_DIM)
    v_blocks = v_slc.reshape(bsz, n_slc, SLC_LEN, HEAD_DIM)
    kw_pad = jnp.pad(k_win, ((0, 0), (WIN, 0), (0, 0)))
    vw_pad = jnp.pad(v_win, ((0, 0), (WIN, 0), (0, 0)))
    q_off = jnp.arange(Q_BLOCK)
    dist_w = q_off[:, None] + WIN - jnp.arange(Q_BLOCK + WIN)[None, :]
    in_win = (dist_w >= 0) & (dist_w < WIN)
    bias_w = jnp.moveaxis(rel_bias_lookup(bias_tab, dist_w), -1, 0)[None]
    gather_blocks = jax.vmap(lambda blocks, ids: blocks[ids])

    def block(i):
        t0 = i * Q_BLOCK
        t = t0 + q_off
        qb = lax.dynamic_slice_in_dim(q, t0, Q_BLOCK, axis=1)
        sb = lax.dynamic_slice_in_dim(sel, t0, Q_BLOCK, axis=1)
        kb = gather_blocks(k_blocks, sb).reshape(bsz, Q_BLOCK, k_sel * SLC_LEN, HEAD_DIM)
        vb = gather_blocks(v_blocks, sb).reshape(bsz, Q_BLOCK, k_sel * SLC_LEN, HEAD_DIM)
        key_pos = (sb[..., None] * SLC_LEN + jnp.arange(SLC_LEN)).reshape(bsz, Q_BLOCK, k_sel * SLC_LEN)
        dist_s = t[None, :, None] - key_pos
        bias_s = jnp.moveaxis(rel_bias_lookup(bias_tab, dist_s), -1, 1)
        s_s = jnp.einsum('bqhd,bqnd->bhqn', qb, kb).astype(jnp.float32) * ATTN_SCALE + bias_s
        p_s = jax.nn.softmax(jnp.where((dist_s >= 0)[:, None], s_s, NEG_INF), axis=-1)
        o_s = jnp.einsum('bhqn,bqnd->bqhd', p_s.astype(vb.dtype), vb)
        kw = lax.dynamic_slice_in_dim(kw_pad, t0, Q_BLOCK + WIN, axis=1)
        vw = lax.dynamic_slice_in_dim(vw_pad, t0, Q_BLOCK + WIN, axis=1)
        valid_w = in_win & (t0 - WIN + jnp.arange(Q_BLOCK + WIN) >= 0)[None, :]
        s_w = jnp.einsum('bqhd,bnd->bhqn', qb, kw).astype(jnp.float32) * ATTN_SCALE + bias_w
        p_w = jax.nn.softmax(jnp.where(valid_w[None, None], s_w, NEG_INF), axis=-1)
        o_w = jnp.einsum('bhqn,bnd->bqhd', p_w.astype(vw.dtype), vw)
        return o_s, o_w

    o_slc, o_win = lax.map(block, jnp.arange(n_blk))
    o_slc, o_win = unblock(o_slc), unblock(o_win)
    g = jax.nn.sigmoid(gate_logits + b_gate).reshape(bsz, seq, 3, B_HEADS)[..., None]
    o = g[:, :, 0] * o_cmp + g[:, :, 1] * o_slc + g[:, :, 2] * o_win
    return o.reshape(bsz, seq, B_OUT)


def forgetting_mixer(q, k, v, f_logits, b_f):
    bsz, seq = q.shape[0], q.shape[1]
    n_blk = seq // Q_BLOCK
    log_f = jax.nn.log_sigmoid(f_logits.astype(jnp.float32) + b_f)
    c = jnp.moveaxis(jnp.cumsum(log_f, axis=1), -1, 1)
    key_pos = jnp.arange(seq)

    def block(i):
        t0 = i * Q_BLOCK
        t = t0 + jnp.arange(Q_BLOCK)
        qb = lax.dynamic_slice_in_dim(q, t0, Q_BLOCK, axis=1)
        c_q = lax.dynamic_slice_in_dim(c, t0, Q_BLOCK, axis=2)
        s = (jnp.einsum('bqhd,bshd->bhqs', qb, k).astype(jnp.float32) * ATTN_SCALE
             + c_q[..., None] - c[:, :, None, :])
        s = jnp.where((key_pos[None, :] <= t[:, None])[None, None], s, NEG_INF)
        p = jax.nn.softmax(s, axis=-1)
        return jnp.einsum('bhqs,bshd->bqhd', p.astype(v.dtype), v)

    o = unblock(lax.map(block, jnp.arange(n_blk)))
    return o.reshape(bsz, seq, C_OUT)


def conv_ffn(x, w_up, conv_w, conv_b, w_down):
    seq = x.shape[1]
    h = jnp.einsum('bsd,df->bsf', x, w_up)
    hp = jnp.pad(h, ((0, 0), (CONV_W - 1, 0), (0, 0)))
    h = conv_b + sum(conv_w[j] * hp[:, j:j + seq] for j in range(CONV_W))
    a, b = jnp.split(h, 2, axis=-1)
    return jnp.einsum('bsf,fd->bsd', jax.nn.gelu(a, approximate=False) * b, w_down)


def hybrid_layer(x, rel_bias, w_in, b_f, b_nsa_gate, cmp_pe, cmp_w1, cmp_b1, cmp_w2, cmp_b2,
                 w_gate, b_gate, w_pa, w_pb, w_pc, w_out, ln1_g, ln1_b,
                 w_up, conv_w, conv_b, w_down, ln2_g, ln2_b):
    bsz, seq, _ = x.shape
    u = split_columns(jnp.einsum('bsd,dn->bsn', x, w_in))

    def heads(t, h):
        return t.reshape(bsz, seq, h, HEAD_DIM)

    o_a = dilated_mixer(heads(u['a_q'], A_HEADS), heads(u['a_k'], A_HEADS), heads(u['a_v'], A_HEADS),
                        rel_bias[:, :A_HEADS])
    o_b = nsa_mixer(heads(u['b_q'], B_HEADS), u['b_k_cmp'], u['b_v_cmp'], u['b_k_slc'], u['b_v_slc'],
                    u['b_k_win'], u['b_v_win'], u['b_gate'], b_nsa_gate,
                    cmp_pe, cmp_w1, cmp_b1, cmp_w2, cmp_b2, rel_bias[:, A_HEADS:])
    o_c = forgetting_mixer(heads(u['c_q'], C_HEADS), heads(u['c_k'], C_HEADS), heads(u['c_v'], C_HEADS),
                           u['c_f'], b_f)
    g_a, g_b, g_c = jnp.split(jax.nn.sigmoid(jnp.einsum('bsd,de->bse', x, w_gate) + b_gate), 3, axis=-1)
    mixed = g_a * (o_a @ w_pa) + g_b * (o_b @ w_pb) + g_c * (o_c @ w_pc)
    x = layer_norm(ALPHA * x + mixed @ w_out, ln1_g, ln1_b)
    x = layer_norm(ALPHA * x + conv_ffn(x, w_up, conv_w, conv_b, w_down), ln2_g, ln2_b)
    return x


def setup_inputs(seed: int = 0) -> dict:
    key = jax.random.key(seed)
    ks = jax.random.split(key, 24)
    f32 = jnp.float32
    L, D, dh = DEPTH, D_MODEL, HEAD_DIM

    def nrm(k, shape, fan_in, gain=1.0):
        return jax.random.normal(k, shape, f32) * (gain * fan_in ** -0.5)

    def small(k, shape, s):
        return jax.random.normal(k, shape, f32) * s

    col_scale = np.concatenate([np.full(w, BETA if n in V_COLUMNS else 1.0, np.float32) for n, w in IN_SPLITS])
    return {
        'x': jax.random.normal(ks[0], (BATCH, SEQ, D), f32),
        'rel_bias': small(ks[1], (N_BUCKETS, BIAS_HEADS), 0.5),
        'w_in': nrm(ks[2], (L, D, N_IN), D) * jnp.asarray(col_scale),
        'b_f': jax.random.uniform(ks[3], (L, C_HEADS), f32, 1.0, 6.0),
        'b_nsa_gate': small(ks[4], (L, 3 * B_HEADS), 0.1),
        'cmp_pe': small(ks[5], (L, 2, CMP_LEN, dh), 0.1),
        'cmp_w1': nrm(ks[6], (L, 2, CMP_LEN * dh, dh), CMP_LEN * dh),
        'cmp_b1': small(ks[7], (L, 2, dh), 0.02),
        'cmp_w2': nrm(ks[8], (L, 2, dh, dh), dh),
        'cmp_b2': small(ks[9], (L, 2, dh), 0.02),
        'w_gate': nrm(ks[10], (L, D, 3 * D), D),
        'b_gate': small(ks[11], (L, 3 * D), 0.1),
        'w_pa': nrm(ks[12], (L, A_OUT, D), A_OUT, BETA),
        'w_pb': nrm(ks[13], (L, B_OUT, D), B_OUT, BETA),
        'w_pc': nrm(ks[14], (L, C_OUT, D), C_OUT, BETA),
        'w_out': nrm(ks[15], (L, D, D), D, BETA),
        'ln1_g': 1.0 + small(ks[16], (L, D), 0.05),
        'ln1_b': small(ks[17], (L, D), 0.02),
        'w_up': nrm(ks[18], (L, D, 2 * D_FF), D, BETA),
        'conv_w': nrm(ks[19], (L, CONV_W, 2 * D_FF), CONV_W),
        'conv_b': small(ks[20], (L, 2 * D_FF), 0.02),
        'w_down': nrm(ks[21], (L, D_FF, D), D_FF, BETA),
        'ln2_g': 1.0 + small(ks[22], (L, D), 0.05),
        'ln2_b': small(ks[23], (L, D), 0.02),
    }


def reference(x, rel_bias, w_in, b_f, b_nsa_gate, cmp_pe, cmp_w1, cmp_b1, cmp_w2, cmp_b2,
              w_gate, b_gate, w_pa, w_pb, w_pc, w_out, ln1_g, ln1_b,
              w_up, conv_w, conv_b, w_down, ln2_g, ln2_b):
    for l in range(DEPTH):
        x = hybrid_layer(x, rel_bias, w_in[l], b_f[l], b_nsa_gate[l], cmp_pe[l], cmp_w1[l], cmp_b1[l],
                         cmp_w2[l], cmp_b2[l], w_gate[l], b_gate[l], w_pa[l], w_pb[l], w_pc[l], w_out[l],
                         ln1_g[l], ln1_b[l], w_up[l], conv_w[l], conv_b[l], w_down[l], ln2_g[l], ln2_b[l])
    return x
```

```python
import math
from contextlib import ExitStack

import numpy as np
import concourse.bass as bass
import concourse.mybir as mybir
from concourse.bass_utils import run_bass_kernel_spmd

F32 = mybir.dt.float32
BF16 = mybir.dt.bfloat16
AF = mybir.ActivationFunctionType
ALU = mybir.AluOpType

S = 2048
D = 2048
DEPTH = 2
DH = 128
N_IN = 5906
D_FF = 5632
NFC = D_FF // 128
ALPHA = (2 * DEPTH) ** 0.25
LN_EPS = 1e-5
SCALE = DH ** -0.5
NEG = -30000.0
N_CMP = 127

C_AQ, C_AK, C_AV, C_BQ = 0, 768, 1536, 2304
C_BKC, C_BVC, C_BKS, C_BVS, C_BKW, C_BVW = 2816, 2944, 3072, 3200, 3328, 3456
C_BG, C_CQ, C_CK, C_CV, C_CF = 3584, 3596, 4364, 5132, 5900

FM_AQ, FM_AK, FM_BQ, FM_BKC, FM_BVC, FM_BKS, FM_BKW, FM_CQ, FM_CK = 0, 6, 12, 16, 17, 18, 19, 20, 26
VT_A, VT_BS, VT_BW, VT_C = 0, 6, 7, 8

ENGS = ("pe", "act", "dve", "pool", "sp")


class Op:
    __slots__ = ("eng", "fn", "deps", "dma", "semkey", "signal", "needed")

    def __init__(self, eng, fn, dma, semkey):
        self.eng = eng
        self.fn = fn
        self.deps = []
        self.dma = dma
        self.semkey = semkey
        self.signal = None
        self.needed = False


class Prog:
    def __init__(self, nc):
        self.nc = nc
        self.ops = {e: [] for e in ENGS}
        self.last_w = {}
        self.readers = {}
        self.bar = {}

    def barrier(self):
        deps = []
        for e in ENGS:
            for op in reversed(self.ops[e]):
                if not op.dma:
                    deps.append(op)
                    break
        lastd = {}
        for e in ENGS:
            for op in self.ops[e]:
                if op.dma:
                    lastd[op.semkey] = op
        deps.extend(lastd.values())
        for d in deps:
            d.needed = True
        self.bar = {e: list(deps) for e in ENGS}
        self.last_w = {}
        self.readers = {}

    def add(self, eng, fn, reads=(), writes=(), dma=False, semkey=None):
        op = Op(eng, fn, dma, semkey)
        if self.bar.get(eng):
            for d in self.bar.pop(eng):
                if d.eng == eng and not d.dma:
                    continue
                op.deps.append(d)
        deps = []
        for r in reads:
            w = self.last_w.get(r)
            if w is not None:
                deps.append(w)
            if isinstance(r, tuple) and r[0] == "ps":
                rd = self.readers.get(r)
                if rd:
                    deps.extend(o for o in rd.values() if o.eng != eng)
        for wk in writes:
            w = self.last_w.get(wk)
            if w is not None:
                deps.append(w)
            rd = self.readers.get(wk)
            if rd:
                deps.extend(rd.values())
        seen = set()
        for d in deps:
            if id(d) in seen or d is op:
                continue
            seen.add(id(d))
            if d.eng == "pe" and eng == "pe" and not d.dma and not dma:
                continue
            op.deps.append(d)
            d.needed = True
        for wk in writes:
            self.last_w[wk] = op
            self.readers[wk] = {}
        for r in reads:
            rd = self.readers.setdefault(r, {})
            if dma:
                rd[("dma", id(op))] = op
            else:
                rd[eng] = op
        self.ops[eng].append(op)
        return op

    def pe(self, fn, reads=(), writes=()):
        return self.add("pe", fn, reads, writes)

    def act(self, fn, reads=(), writes=()):
        return self.add("act", fn, reads, writes)

    def dve(self, fn, reads=(), writes=()):
        return self.add("dve", fn, reads, writes)

    def pool(self, fn, reads=(), writes=()):
        return self.add("pool", fn, reads, writes)

    def dma(self, queue, out, in_, reads=(), writes=(), semkey=None, join=False, **kw):
        op = self.add(queue, lambda e: e.dma_start(out=out, in_=in_, **kw), reads, writes,
                      dma=True, semkey=semkey)
        if join or (isinstance(semkey, str) and semkey.startswith("grp_")):
            op.deps = [d for d in op.deps if not (d.dma and d.semkey == semkey and d.eng == queue)]
        return op

    def emit(self, final_ops=()):
        nc = self.nc
        semkeys = []
        skset = set()
        for e in ENGS:
            for op in self.ops[e]:
                if op.dma and op.semkey not in skset:
                    skset.add(op.semkey)
                    semkeys.append(op.semkey)
        with ExitStack() as st:
            prog_sem = {e: st.enter_context(nc.semaphore("prog_" + e)) for e in ENGS}
            dma_sem = {k: st.enter_context(nc.semaphore("dma_%d" % i)) for i, k in enumerate(semkeys)}
            cnt = {e: 0 for e in ENGS}
            dcnt = {k: 0 for k in semkeys}
            for e in ENGS:
                for op in self.ops[e]:
                    if op.dma:
                        dcnt[op.semkey] += 16
                        op.signal = (dma_sem[op.semkey], dcnt[op.semkey])
                    elif op.needed:
                        cnt[e] += 1
                        op.signal = (prog_sem[e], cnt[e])
            for e in ENGS:
                for op in self.ops[e]:
                    if op.dma and isinstance(op.semkey, str) and op.semkey.startswith("grp_"):
                        op.signal = (dma_sem[op.semkey], dcnt[op.semkey])
            block = st.enter_context(nc.Block())
            engmap = {"pe": block.tensor, "act": block.scalar, "dve": block.vector,
                      "pool": block.gpsimd, "sp": block.sync}
            stats = {}

            def mk(e):
                def body(eng):
                    seen = {}
                    nw = 0
                    for op in self.ops[e]:
                        for d in op.deps:
                            sem, val = d.signal
                            k = id(sem)
                            if seen.get(k, 0) >= val:
                                continue
                            seen[k] = val
                            eng.wait_ge(sem, val)
                            nw += 1
                        ins = op.fn(eng)
                        if op.signal is not None:
                            ins.then_inc(op.signal[0], 16 if op.dma else 1)
                    if e == "sp":
                        for d in final_ops:
                            sem, val = d.signal
                            eng.wait_ge(sem, val)
                    stats[e] = (len(self.ops[e]), nw)
                return body

            for e in ENGS:
                engmap[e](mk(e))
        return stats


def _t5_bucket(dist):
    n_buckets, max_distance = 32, 2048
    max_exact = n_buckets // 2
    d = np.maximum(np.asarray(dist, np.int32), 0)
    ratio = np.maximum(d, 1).astype(np.float32) / np.float32(max_exact)
    log_ratio = np.log(ratio).astype(np.float32) / np.float32(math.log(max_distance / max_exact))
    large = np.minimum(max_exact + (log_ratio * np.float32(n_buckets - max_exact)).astype(np.int32), n_buckets - 1)
    return np.where(d < max_exact, d, large)


L_A, L_S, L_W, L_C = 1152, 3072, 1536, 4096
M_A, M_S, M_W, M_C = 1024, 2816, 1408, 2048


def _onehot(L, off, dil, lo, hi):
    i = np.arange(L)
    d = i - off
    valid = (d >= lo) & (d <= hi)
    b = _t5_bucket(np.clip(d, 0, None) * dil)
    oh = np.zeros((33, L), np.float32)
    oh[b[valid], i[valid]] = 1.0
    oh[32, i[~valid]] = 1.0
    return oh


_CONSTS = None


def host_consts():
    global _CONSTS
    if _CONSTS is not None:
        return _CONSTS
    c = {}
    c["ident"] = np.eye(128, dtype=np.float32)
    oh_a = np.stack([_onehot(L_A, 511, dil, 0, 128) for dil in (1, 4, 16)])
    c["oh_a"] = np.ascontiguousarray(oh_a.transpose(1, 0, 2))
    c["oh_s"] = _onehot(L_S, 511, 1, 0, 2047)
    c["oh_w"] = _onehot(L_W, 511, 1, 0, 511)
    c["oh_c"] = _onehot(L_C, 2047, 1, 0, 2047)
    p = np.arange(128)[:, None]
    m = np.arange(1024)[None, :]
    c["cmask"] = np.where(m - 384 - p >= 0, 0.0, NEG).astype(np.float32)
    ci = np.arange(N_CMP)[:, None]
    sj = np.arange(32)[None, :]
    ov = ((ci * 16 <= sj * 64 + 63) & (ci * 16 + 31 >= sj * 64)).astype(np.float32)
    c["ovx"] = np.concatenate([ov, np.ones((N_CMP, 1), np.float32)], axis=1)
    c["eall"] = (np.arange(S)[None, :] // 64 == np.arange(32)[:, None]).astype(np.float32)
    pos = np.arange(S)[:, None]
    blk = np.arange(32)[None, :]
    cur = pos // 64
    forced = (blk == 0) | (blk == cur) | (blk == cur - 1)
    causal = blk * 64 <= pos
    selmul = (causal & ~forced).astype(np.float32)
    seladd = np.where(forced, 1e9, np.where(causal, 0.0, -1e30)).astype(np.float32)
    c["selmul"] = np.ascontiguousarray(selmul.reshape(16, 128, 32).transpose(1, 0, 2))
    c["seladd"] = np.ascontiguousarray(seladd.reshape(16, 128, 32).transpose(1, 0, 2))
    sel12 = np.zeros((12, 12, 128), np.float32)
    for h in range(12):
        sel12[h, h, :] = 1.0
    c["sel12"] = sel12
    _CONSTS = c
    return c


CONST_SHAPES = {
    "ident": [128, 128], "oh_a": [33, 3, L_A], "oh_s": [33, L_S], "oh_w": [33, L_W], "oh_c": [33, L_C],
    "cmask": [128, 1024], "ovx": [N_CMP, 33], "eall": [32, S], "selmul": [128, 16, 32],
    "seladd": [128, 16, 32], "sel12": [12, 12, 128],
}

WEIGHT_SHAPES = {
    "rel_bias": [32, 10], "w_in": [2, D, N_IN], "b_f": [2, 6], "b_nsa_gate": [2, 12],
    "cmp_pe": [2, 2, 32, 128], "cmp_w1": [2, 2, 4096, 128], "cmp_b1": [2, 2, 128],
    "cmp_w2": [2, 2, 128, 128], "cmp_b2": [2, 2, 128], "w_gate": [2, D, 3 * D], "b_gate": [2, 3 * D],
    "w_pa": [2, 256, D], "w_pb": [2, 512, D], "w_pc": [2, 768, D], "w_out": [2, D, D],
    "ln1_g": [2, D], "ln1_b": [2, D], "w_up": [2, D, 2 * D_FF], "conv_w": [2, 3, 2 * D_FF],
    "conv_b": [2, 2 * D_FF], "w_down": [2, D_FF, D], "ln2_g": [2, D], "ln2_b": [2, D],
}


class Builder:
    def __init__(self, n_layers=DEPTH, first_layer=0, do_setup=True, dbg=()):
        self.n_layers = n_layers
        self.first_layer = first_layer
        self.dbg = set(dbg)
        self.nc = nc = bass.Bass("TRN2", target_bir_lowering=False)
        self.P = Prog(nc)
        self.st = ExitStack()
        self.dmac = 0
        d = {}
        d["x"] = nc.dram_tensor("x", [S, D], F32, kind="ExternalInput").ap()
        for k, shp in WEIGHT_SHAPES.items():
            d[k] = nc.dram_tensor(k, shp, F32, kind="ExternalInput").ap()
        for k, shp in CONST_SHAPES.items():
            d[k] = nc.dram_tensor("c_" + k, shp, F32, kind="ExternalInput").ap()
        d["out"] = nc.dram_tensor("out", [S, D], F32, kind="ExternalOutput").ap()
        self.d = d
        self.outs = []
        self.s = {}
        self.sh = {}

        def scr(name, shape, dt):
            kind = "ExternalOutput" if name in self.dbg else "Internal"
            h = nc.dram_tensor("s_" + name, shape, dt, kind=kind)
            self.sh[name] = h
            self.s[name] = h.ap()
        scr("xres", [16, 128, S], F32)
        scr("x1f", [16, 128, S], F32)
        scr("xb", [16, 128, S], BF16)
        scr("x1b", [16, 128, S], BF16)
        scr("fm", [32, 128, S], BF16)
        scr("vt", [14, S, 128], BF16)
        scr("gt", [48, 128, S], BF16)
        scr("ot", [12, 128, S], BF16)
        scr("zt", [NFC, 128, S], BF16)
        scr("spec", [2, 12, S], F32)
        scr("wdc", [16, 128, NFC * 128], BF16)
        scr("fa", [6, 128, L_A], F32)
        scr("fs", [4, 128, L_S], F32)
        scr("fw", [4, 128, L_W], F32)
        scr("fc", [4, 128, L_C], F32)

    def sb(self, name, shape, dt):
        return self.st.enter_context(self.nc.sbuf_tensor(name, shape, dt))

    def alloc(self):
        nc = self.nc
        st = self.st
        self.X = self.sb("X", [128, 16384], F32)
        self.Z = self.sb("Z", [128, 12288], F32)
        self.ps = [st.enter_context(nc.psum_tensor("ps%d" % i, [128, 512], F32)) for i in range(8)]
        self.wb = [self.sb("wb%d" % i, [128, 6144], BF16) for i in range(2)]
        self.ident = self.sb("ident", [128, 128], F32)
        self.ones_bf = self.sb("ones_bf", [128, 128], BF16)
        self.ones_f = self.sb("ones_f", [128, 128], F32)
        self.cmask = self.sb("cmask", [128, 1024], F32)
        self.eall = self.sb("eall", [32, S], BF16)
        self.ovx = self.sb("ovx", [128, 33], BF16)
        self.selmul = self.sb("selmul", [128, 16, 32], F32)
        self.seladd = self.sb("seladd", [128, 16, 32], F32)
        self.sel12 = self.sb("sel12", [12, 12, 128], F32)
        self.selbT = self.sb("selbT", [32, S], BF16)
        self.imp = self.sb("imp", [128, 16, 32], F32)
        self.negck = self.sb("negck", [128, 16, 6], F32)
        self.cols = self.sb("cols", [128, 512], F32)
        self.kcT = self.sb("kcT", [128, 128], BF16)
        self.vc = self.sb("vc", [128, 128], BF16)
        self.small = self.sb("small", [128, 64], F32)
        self.spw = self.sb("spw", [128, 16, 18], F32)
        self.tiny = self.sb("tiny", [128, 2], F32)
        self.tf = [self.sb("tf%d" % i, [128, 512], F32) for i in range(2)]
        self.pT = [self.sb("pT%d" % i, [128, 512], BF16) for i in range(3)]
        self.sf = [self.sb("sf%d" % i, [128, 512], F32) for i in range(2)]
        self.sbf = [self.sb("sbf%d" % i, [128, 512], BF16) for i in range(3)]
        self.rs = [self.sb("rs%d" % i, [128, 512], F32) for i in range(2)]
        self.hraw = [self.sb("hraw%d" % i, [128, 514], F32) for i in range(2)]
        self.u = [self.sb("u%d" % i, [128, 512], F32) for i in range(6)]
        self.rr = {}
        self.pend = []

    def rot(self, name, n):
        v = self.rr.get(name, 0)
        self.rr[name] = v + 1
        return v % n

    def xT(self):
        return self.X[:].bitcast(BF16).rearrange("p (k t) -> p k t", k=16)

    def Zf(self, off, n):
        return self.Z[:, off:off + n]

    def Zb(self, off_f32, n_bf):
        return self.Z[:, off_f32:off_f32 + n_bf // 2].bitcast(BF16)

    def mm(self, ps_ap, pskey, lhsT, rhs, start, stop, reads):
        self.P.pe(lambda e: e.matmul(out=ps_ap, lhsT=lhsT, rhs=rhs, start=start, stop=stop),
                  reads=reads, writes=[pskey])

    def tr(self, ps_ap, pskey, in_, n_in_part, reads):
        ident = self.ident[0:n_in_part, 0:n_in_part]
        self.P.pe(lambda e: e.transpose(out=ps_ap, in_=in_, identity=ident),
                  reads=list(reads) + ["ident"], writes=[pskey])

    def load_cols(self, col0, vec_ap, n):
        P = self.P
        stg = self.sf[1]
        P.dma("sp", stg[0:n, 0:128], vec_ap.rearrange("(n p) -> n p", p=128), writes=[("sf", 1)], semkey="sfld")
        b = 7
        self.tr(self.ps[b][:, 0:n], ("ps", b), stg[0:n, 0:128], n, [("sf", 1)])
        cols = self.cols
        P.act(lambda e: e.copy(out=cols[:, col0:col0 + n], in_=self.ps[b][:, 0:n]),
              reads=[("ps", b)], writes=["cols"])

    def setup(self):
        P, d = self.P, self.d
        g = "grp_setup"
        P.dma("sp", self.ident[:], d["ident"], writes=["ident"], semkey=g)
        P.dma("sp", self.cmask[:], d["cmask"], writes=["cmask"], semkey=g)
        P.dma("sp", self.selmul[:], d["selmul"], writes=["selmul"], semkey=g)
        P.dma("sp", self.seladd[:], d["seladd"], writes=["seladd"], semkey=g)
        P.dma("sp", self.sel12[:], d["sel12"], writes=["sel12"], semkey=g)
        P.dma("pool", self.eall[:], d["eall"], writes=["eall"], semkey="grp_setup_p")
        P.dma("pool", self.ovx[0:N_CMP, :], d["ovx"], writes=["ovx"], semkey="grp_setup_p")
        P.dve(lambda e: e.memset(self.ones_bf[:], 1.0), writes=["ones_bf"])
        P.dve(lambda e: e.memset(self.tiny[:], 1e-30), writes=["tiny"])
        P.dve(lambda e: e.memset(self.ones_f[:], 1.0), writes=["ones_f"])
        P.dve(lambda e: e.memset(self.selbT[:], 0.0), writes=["selbT"])
        X = self.X
        tbl = X[0:32, 0:10]
        lhs_all = X[0:33, 16:16 + 1280].rearrange("p (h m) -> p h m", h=10)
        oh_a = X[0:33, 1536:1536 + 3 * L_A].rearrange("p (g l) -> p g l", g=3)
        o1 = 1536 + 3 * L_A
        oh_s = X[0:33, o1:o1 + L_S]
        oh_w = X[0:33, o1 + L_S:o1 + L_S + L_W]
        oh_c = X[0:33, o1 + L_S + L_W:o1 + L_S + L_W + L_C]
        P.dma("sp", tbl, d["rel_bias"], writes=["tbl"], semkey=g)
        P.dma("sp", oh_a, d["oh_a"], writes=["oh"], semkey=g)
        P.dma("sp", oh_s, d["oh_s"], writes=["oh"], semkey=g)
        P.dma("sp", oh_w, d["oh_w"], writes=["oh"], semkey=g)
        P.dma("sp", oh_c, d["oh_c"], writes=["oh"], semkey=g)
        P.dve(lambda e: e.memset(X[32:33, 16:16 + 1280], NEG), writes=["lhs_neg"])
        for h in range(10):
            P.dve(lambda e, h=h: e.tensor_copy(out=lhs_all[0:32, h, :], in_=tbl[:, h:h + 1].to_broadcast([32, 128])),
                  reads=["tbl"], writes=[("lhs", h)])
        jobs = []
        for h in range(6):
            jobs.append((h, oh_a[:, h // 2, :], L_A, self.s["fa"][h]))
        for h in range(4):
            jobs.append((6 + h, oh_s, L_S, self.s["fs"][h]))
            jobs.append((6 + h, oh_w, L_W, self.s["fw"][h]))
            jobs.append((6 + h, oh_c, L_C, self.s["fc"][h]))
        for (h, oh, L, dst) in jobs:
            for c0 in range(0, L, 512):
                n = min(512, L - c0)
                b = self.rot("ps8", 8)
                self.mm(self.ps[b][:, 0:n], ("ps", b), lhs_all[0:33, h, :], oh[:, c0:c0 + n], True, True,
                        [("lhs", h), "lhs_neg", "oh"])
                si = self.rot("sf", 2)
                sf = self.sf[si]
                P.act(lambda e, sf=sf, b=b, n=n: e.copy(out=sf[:, 0:n], in_=self.ps[b][:, 0:n]),
                      reads=[("ps", b)], writes=[("sf", si)])
                P.dma("sp", dst[:, c0:c0 + n], sf[:, 0:n], reads=[("sf", si)], writes=["ftab"], semkey="sf%d" % si)

    def tab_src(self, name, h, L, s, C, M):
        hd = self.sh[name]
        return bass.AP(tensor=hd, offset=h * 128 * L + C, ap=[[L - s, 128], [1, M]])

    def special_proj(self, l, xf, xfreads, t0, n):
        P = self.P
        for (j, nc_, c0) in ((0, 12, 0), (1, 6, 12)):
            b = self.rot("ps8", 8)
            for kc in range(16):
                self.mm(self.ps[b][0:nc_, 0:n], ("ps", b), self.spw[:, kc, c0:c0 + nc_], xf[:, kc, 0:n],
                        kc == 0, kc == 15, [("spw", q, jj) for q in range(4) for jj in range(2)] + xfreads)
            si = self.rot("sf", 2)
            sf = self.sf[si]
            P.act(lambda e, sf=sf, b=b, nc_=nc_: e.copy(out=sf[0:nc_, 0:n], in_=self.ps[b][0:nc_, 0:n]),
                  reads=[("ps", b)], writes=[("sf", si)])
            P.dma("sp", self.s["spec"][j, 0:nc_, t0:t0 + n], sf[0:nc_, 0:n], reads=[("sf", si)],
                  writes=[("spec", j, t0)], semkey="sf%d" % si)

    def load_spw(self, l):
        P, d = self.P, self.d
        for q in range(4):
            ks = slice(q * 512, (q + 1) * 512)
            P.dma("sp", self.spw[:, 4 * q:4 * q + 4, 0:12],
                  d["w_in"][l, ks, C_BG:C_BG + 12].rearrange("(k p) n -> p k n", p=128),
                  writes=[("spw", q, 0)], semkey="grp_spw%d" % l)
            P.dma("sp", self.spw[:, 4 * q:4 * q + 4, 12:18],
                  d["w_in"][l, ks, C_CF:C_CF + 6].rearrange("(k p) n -> p k n", p=128),
                  writes=[("spw", q, 1)], semkey="grp_spw%d" % l)

    def phaseA(self, l):
        P, d = self.P, self.d
        xT = self.xT()
        xin = self.Zf(0, 4096).rearrange("p (c d) -> p c d", c=2)
        xf = self.Zf(4096, 4096).rearrange("p (k t) -> p k t", k=16)
        import os
        nospec = os.environ.get("NOSPEC") == "1"
        if not nospec:
            self.load_spw(l)
        for t8 in range(8):
            t0 = t8 * 256
            P.dma("sp", xin, d["x"][t0:t0 + 256, :].rearrange("(c p) d -> p c d", p=128), writes=["xin"], semkey="xin")
            for b2 in range(8):
                b = self.rot("ps8", 8)
                for j in range(2):
                    dc = 2 * b2 + j
                    for c in range(2):
                        self.tr(self.ps[b][:, j * 256 + c * 128:j * 256 + (c + 1) * 128], ("ps", b),
                                xin[:, c, dc * 128:(dc + 1) * 128], 128, ["xin"])
                psv = self.ps[b][:, 0:512].rearrange("p (j t) -> p j t", j=2)
                P.act(lambda e, psv=psv, b2=b2: e.copy(out=xf[:, 2 * b2:2 * b2 + 2, :], in_=psv),
                      reads=[("ps", b)], writes=[("xf", b2)])
                P.dve(lambda e, psv=psv, b2=b2, t0=t0: e.tensor_copy(out=xT[:, 2 * b2:2 * b2 + 2, t0:t0 + 256], in_=psv),
                      reads=[("ps", b)], writes=[("xT", t8)])
            xfr = [("xf", i) for i in range(8)]
            P.dma("sp", self.s["xres"][:, :, t0:t0 + 256].rearrange("k p t -> p k t"), xf, reads=xfr,
                  writes=[("xres", t8 // 2)], semkey="xf_st")
            if not nospec:
                self.special_proj(l, xf, xfr, t0, 256)

    def xperm(self, g, kc, tb):
        xT = self.xT()
        if g == 0:
            return xT[:, kc, tb * 512:(tb + 1) * 512]
        if g == 1:
            return xT[:, kc, :].rearrange("p (i r) -> p r i", r=4)[:, tb, :]
        return xT[:, kc, :].rearrange("p (i r) -> p r i", r=16)[:, 4 * tb:4 * tb + 4, :]

    def xperm_chunk(self, g, kc, c):
        xT = self.xT()
        if g == 0:
            return xT[:, kc, c * 128:(c + 1) * 128]
        if g == 1:
            return xT[:, kc, :].rearrange("p (i r) -> p r i", r=4)[:, c // 4, (c % 4) * 128:(c % 4 + 1) * 128]
        return xT[:, kc, :].rearrange("p (i r) -> p r i", r=16)[:, c, :]

    def wload(self, src_ap, K, ncols, tag):
        si = self.rot("wb", 2)
        wv = self.wb[si][:, 0:K * ncols].rearrange("p (k n) -> p k n", k=K)
        self.P.dma("pool", wv, src_ap.rearrange("(k p) n -> p k n", p=128), writes=[("wb", si)], semkey="wb%d" % si)
        return wv, ("wb", si)

    def phaseB(self, l):
        P, d = self.P, self.d
        w_in = d["w_in"]
        xall = [("xT", i) for i in range(8)]
        fm_groups = []
        for i in range(3):
            fm_groups.append((C_AQ + 256 * i, 256, [FM_AQ + 2 * i, FM_AQ + 2 * i + 1], i))
        for i in range(3):
            fm_groups.append((C_AK + 256 * i, 256, [FM_AK + 2 * i, FM_AK + 2 * i + 1], i))
        for i in range(2):
            fm_groups.append((C_BQ + 256 * i, 256, [FM_BQ + 2 * i, FM_BQ + 2 * i + 1], 0))
        fm_groups.append((C_BKC, 256, [FM_BKC, FM_BVC], 0))
        fm_groups.append((C_BKS, 128, [FM_BKS], 0))
        fm_groups.append((C_BKW, 128, [FM_BKW], 0))
        for i in range(3):
            fm_groups.append((C_CQ + 256 * i, 256, [FM_CQ + 2 * i, FM_CQ + 2 * i + 1], 0))
        for i in range(3):
            fm_groups.append((C_CK + 256 * i, 256, [FM_CK + 2 * i, FM_CK + 2 * i + 1], 0))
        for (c0, ncol, blks, g) in fm_groups:
            wv, wkey = self.wload(w_in[l, :, c0:c0 + ncol], 16, ncol, "fm")
            for j, blk in enumerate(blks):
                for tb in range(4):
                    b = self.rot("ps8", 8)
                    for kc in range(16):
                        self.mm(self.ps[b][:, 0:512], ("ps", b), wv[:, kc, j * 128:(j + 1) * 128], self.xperm(g, kc, tb),
                                kc == 0, kc == 15, [wkey] + xall)
                    si = self.rot("sbf", 3)
                    sbf = self.sbf[si]
                    P.act(lambda e, sbf=sbf, b=b: e.copy(out=sbf[:, :], in_=self.ps[b][:, :]),
                          reads=[("ps", b)], writes=[("sbf", si)])
                    P.dma("sp", self.s["fm"][blk, :, tb * 512:(tb + 1) * 512], sbf[:, :], reads=[("sbf", si)],
                          writes=[("fm", blk, tb)], semkey="sbf%d" % si)
        tm_groups = [(C_AV, [0, 1], 0), (C_AV + 256, [2, 3], 1), (C_AV + 512, [4, 5], 2), (None, [6, 7], 0),
                     (C_CV, [8, 9], 0), (C_CV + 256, [10, 11], 0), (C_CV + 512, [12, 13], 0)]
        for (c0, hds, g) in tm_groups:
            if c0 is not None:
                wv, wkey = self.wload(w_in[l, :, c0:c0 + 256], 16, 256, "tm")
            else:
                si = self.rot("wb", 2)
                wv = self.wb[si][:, 0:16 * 256].rearrange("p (k n) -> p k n", k=16)
                wkey = ("wb", si)
                P.dma("pool", wv[:, :, 0:128], w_in[l, :, C_BVS:C_BVS + 128].rearrange("(k p) n -> p k n", p=128),
                      writes=[wkey], semkey="wb%d" % si)
                P.dma("pool", wv[:, :, 128:256], w_in[l, :, C_BVW:C_BVW + 128].rearrange("(k p) n -> p k n", p=128),
                      writes=[wkey], semkey="wb%d" % si, join=True)
            for c in range(16):
                b = self.rot("ps8", 8)
                for kc in range(16):
                    self.mm(self.ps[b][:, 0:256], ("ps", b), self.xperm_chunk(g, kc, c), wv[:, kc, :],
                            kc == 0, kc == 15, [wkey] + xall)
                si = self.rot("sbf", 3)
                sbf = self.sbf[si]
                P.act(lambda e, sbf=sbf, b=b: e.copy(out=sbf[:, 0:256], in_=self.ps[b][:, 0:256]),
                      reads=[("ps", b)], writes=[("sbf", si)])
                P.dma("sp", self.s["vt"][hds[0]:hds[0] + 2, c * 128:(c + 1) * 128, :].rearrange("h p e -> p h e"),
                      sbf[:, 0:256].rearrange("p (h e) -> p h e", h=2), reads=[("sbf", si)],
                      writes=[("vt", hds[0]), ("vt", hds[1])], semkey="sbf%d" % si)
        self.load_cols(0, d["b_gate"][l], 48)
        for gi in range(24):
            wv, wkey = self.wload(d["w_gate"][l, :, gi * 256:(gi + 1) * 256], 16, 256, "gate")
            for j in range(2):
                blk = 2 * gi + j
                for tb in range(4):
                    b = self.rot("ps8", 8)
                    for kc in range(16):
                        self.mm(self.ps[b][:, 0:512], ("ps", b), wv[:, kc, j * 128:(j + 1) * 128], self.xperm(0, kc, tb),
                                kc == 0, kc == 15, [wkey] + xall)
                    si = self.rot("sbf", 3)
                    sbf = self.sbf[si]
                    P.act(lambda e, sbf=sbf, b=b, blk=blk: e.activation(out=sbf[:, :], in_=self.ps[b][:, :], func=AF.Sigmoid,
                                                                      bias=self.cols[:, blk:blk + 1]),
                          reads=[("ps", b), "cols"], writes=[("sbf", si)])
                    P.dma("sp", self.s["gt"][blk, :, tb * 512:(tb + 1) * 512], sbf[:, :], reads=[("sbf", si)],
                          writes=[("gt", blk, tb)], semkey="sbf%d" % si)

    def Xv(self, lo, n, parts=128):
        return self.X[0:parts, lo:lo + n]

    def phaseS(self, l):
        P, d = self.P, self.d
        xall = [("xT", i) for i in range(8)]
        gsig = self.Xv(12288, S, 12)
        cT = self.Xv(14336, S, 6)
        l1 = self.Xv(8192, S, 6)
        sm = self.small
        spec_r = [("spec", j, t0) for j in range(2) for t0 in range(0, S, 256)] + \
                 [("spec", j, t0) for j in range(2) for t0 in range(0, S, 512)]
        P.dma("sp", gsig, self.s["spec"][0, 0:12, :], reads=spec_r, writes=["gsig"] + xall, semkey="specld0")
        P.dma("sp", l1, self.s["spec"][1, 0:6, :], reads=spec_r, writes=["l1"] + xall, semkey="specld1")
        P.dma("sp", sm[0:12, 0:1], d["b_nsa_gate"][l].rearrange("(p o) -> p o", o=1), writes=["sm0"], semkey="sm_a")
        P.dma("sp", sm[0:6, 1:2], d["b_f"][l].rearrange("(p o) -> p o", o=1), writes=["sm1"], semkey="sm_b")
        P.act(lambda e: e.activation(out=gsig, in_=gsig, func=AF.Sigmoid, bias=sm[0:12, 0:1]),
              reads=["gsig", "sm0"], writes=["gsig"])
        P.dve(lambda e: e.tensor_scalar(out=sm[0:6, 2:3], in0=sm[0:6, 1:2], scalar1=-1.0, scalar2=None, op0=ALU.mult),
              reads=["sm1"], writes=["sm2"])
        P.act(lambda e: e.activation(out=l1, in_=l1, func=AF.Exp, bias=sm[0:6, 2:3], scale=-1.0),
              reads=["l1", "sm2"], writes=["l1"])
        P.act(lambda e: e.activation(out=l1, in_=l1, func=AF.Ln, bias=1.0), reads=["l1"], writes=["l1"])
        P.dve(lambda e: e.tensor_tensor_scan(out=cT, data0=self.ones_f[0:6, 0:1].to_broadcast([6, S]), data1=l1,
                                             initial=0.0, op0=ALU.mult, op1=ALU.subtract),
              reads=["l1", "ones_f"], writes=["cT"] + xall)
        b = self.rot("ps8", 8)
        for c in range(16):
            self.tr(self.ps[b][:, c * 6:(c + 1) * 6], ("ps", b), cT[0:6, c * 128:(c + 1) * 128], 6, ["cT"])
        P.act(lambda e: e.mul(out=self.negck[:].rearrange("p c h -> p (c h)"), in_=self.ps[b][:, 0:96], mul=-1.0),
              reads=[("ps", b)], writes=["negck"])

    def abuf(self):
        q = [self.Zb(0, 2048), self.Zb(1024, 2048)]
        k = [self.Zb(2048, 2048), self.Zb(3072, 2048)]
        v = [self.Zb(4096, 2048).rearrange("p (c e) -> p c e", c=16), self.Zb(5120, 2048).rearrange("p (c e) -> p c e", c=16)]
        tab = [self.Zf(6144, 2816), self.Zf(8960, 2816)]
        return q, k, v, tab

    def ld_fm(self, kind, blk):
        q, k, v, tab = self.abuf()
        bufs = q if kind == "q" else k
        si = self.rot("ab_" + kind, 2)
        self.P.dma("sp", bufs[si], self.s["fm"][blk], reads=[("fm", blk, tb) for tb in range(4)],
                   writes=[(kind, si)], semkey="%s%d" % (kind, si))
        return bufs[si], (kind, si)

    def ld_v(self, hd):
        q, k, v, tab = self.abuf()
        si = self.rot("ab_v", 2)
        self.P.dma("sp", v[si], self.s["vt"][hd].rearrange("(c p) e -> p c e", p=128), reads=[("vt", hd)],
                   writes=[("v", si)], semkey="v%d" % si)
        return v[si], ("v", si)

    def ld_tab(self, name, h, L, s_, C, M):
        q, k, v, tab = self.abuf()
        si = self.rot("ab_t", 2)
        self.P.dma("sp", tab[si][:, 0:M], self.tab_src(name, h, L, s_, C, M), reads=["ftab"],
                   writes=[("tab", si)], semkey="tab%d" % si)
        return tab[si], ("tab", si)

    def phaseM(self, l):
        P, d = self.P, self.d
        sm = self.small
        hid = [self.sbf[0], self.sbf[1]]
        for c in range(2):
            src, skey = self.ld_fm("q", FM_BKC + c)
            si = self.rot("wb", 2)
            w1v = self.wb[si][:, 0:4096].rearrange("p (l e) -> p l e", l=32)
            P.dma("pool", w1v, d["cmp_w1"][l, c].rearrange("(l d) e -> d l e", d=128), writes=[("wb", si)],
                  semkey="wb%d" % si)
            P.dma("sp", self.sf[1][0:32, 0:128], d["cmp_pe"][l, c], writes=[("sf", 1)], semkey="sfld")
            b = self.rot("ps8", 8)
            self.tr(self.ps[b][:, 0:32], ("ps", b), self.sf[1][0:32, 0:128], 32, [("sf", 1)])
            peT = self.sbf[2][:, 0:32]
            P.act(lambda e, b=b: e.copy(out=peT, in_=self.ps[b][:, 0:32]), reads=[("ps", b)], writes=[("sbf", 2)])
            P.dma("sp", sm[:, 4 + c:5 + c], d["cmp_b1"][l, c].rearrange("(p o) -> p o", o=1), writes=[("smb", c)],
                  semkey="sm_c%d" % c)
            bh = self.rot("ps8", 8)
            for li in range(32):
                self.mm(self.ps[bh][:, 0:N_CMP], ("ps", bh), w1v[:, li, :], src[:, li:li + 2017:16], li == 0, li == 31,
                        [("wb", si), skey])
            bp = self.rot("ps8", 8)
            for li in range(32):
                self.mm(self.ps[bp][:, 0:1], ("ps", bp), w1v[:, li, :], peT[:, li:li + 1], li == 0, li == 31,
                        [("wb", si), ("sbf", 2)])
            P.dve(lambda e, bp=bp, c=c: e.tensor_tensor(out=sm[:, 8 + c:9 + c], in0=self.ps[bp][:, 0:1], in1=sm[:, 4 + c:5 + c],
                                                       op=ALU.add), reads=[("ps", bp), ("smb", c)], writes=[("smh", c)])
            P.act(lambda e, bh=bh, c=c: e.activation(out=hid[c][:, 0:N_CMP], in_=self.ps[bh][:, 0:N_CMP], func=AF.Gelu,
                                                    bias=sm[:, 8 + c:9 + c]),
                  reads=[("ps", bh), ("smh", c)], writes=[("sbf", c)])
        si = self.rot("wb", 2)
        w2 = self.wb[si][:, 0:256].rearrange("p (c e) -> p c e", c=2)
        P.dma("pool", w2, d["cmp_w2"][l].rearrange("c e x -> e c x"), writes=[("wb", si)], semkey="wb%d" % si)
        P.dma("sp", sm[:, 6:7], d["cmp_b2"][l, 0].rearrange("(p o) -> p o", o=1), writes=["smb2"], semkey="sm_d")
        b2row = self.sbf[2][0:1, 128:256]
        P.dma("pool", b2row, d["cmp_b2"][l, 1].rearrange("(o e) -> o e", o=1), writes=[("sbf", 2)], semkey="b2row")
        b = self.rot("ps8", 8)
        self.mm(self.ps[b][:, 0:N_CMP], ("ps", b), w2[:, 0, :], hid[0][:, 0:N_CMP], True, True, [("wb", si), ("sbf", 0)])
        P.act(lambda e, b=b: e.activation(out=self.kcT[:, 0:N_CMP], in_=self.ps[b][:, 0:N_CMP], func=AF.Identity,
                                          bias=sm[:, 6:7]), reads=[("ps", b), "smb2"], writes=["kcT"])
        b = self.rot("ps8", 8)
        self.mm(self.ps[b][0:N_CMP, 0:128], ("ps", b), hid[1][:, 0:N_CMP], w2[:, 1, :], True, False, [("wb", si), ("sbf", 1)])
        self.mm(self.ps[b][0:N_CMP, 0:128], ("ps", b), self.ones_bf[0:1, 0:N_CMP], b2row, False, True,
                [("sbf", 2), "ones_bf"])
        P.act(lambda e, b=b: e.copy(out=self.vc[0:N_CMP, :], in_=self.ps[b][0:N_CMP, 0:128]), reads=[("ps", b)],
              writes=["vc"])

    def attn_tile(self, lhsT_k, rhs_q, nk, nq, bias_ap, v_lhsT, ao, as_, first, last, reads_kq, reads_b, reads_v,
                  keybias=None, kb_reads=(), extra=None, sel=None, post=None):
        P = self.P
        sbk = self.sbank()
        ps_s = self.ps[sbk][0:nk, 0:nq]
        self.mm(ps_s, ("ps", sbk), lhsT_k, rhs_q, True, sel is None, reads_kq)
        if sel is not None:
            e_ap, selb_ap, sreads = sel
            self.mm(ps_s, ("ps", sbk), e_ap, selb_ap, False, True, sreads)
        ti = self.rot("tf", 3)
        tf = self.atf()[ti][0:nk, 0:nq]
        P.dve(lambda e: e.scalar_tensor_tensor(out=tf, in0=ps_s, scalar=SCALE, in1=bias_ap, op0=ALU.mult, op1=ALU.add),
              reads=[("ps", sbk)] + list(reads_b), writes=[("tf", ti)])
        if extra is not None:
            m_ap, mreads = extra
            P.dve(lambda e: e.tensor_tensor(out=tf, in0=tf, in1=m_ap, op=ALU.add), reads=[("tf", ti)] + list(mreads),
                  writes=[("tf", ti)])
        pi = self.rot("pT", 7)
        pT = self.apT()[pi][0:nk, 0:nq]
        if keybias is None:
            P.act(lambda e: e.activation(out=pT, in_=tf, func=AF.Exp), reads=[("tf", ti)], writes=[("pT", pi)])
        else:
            P.act(lambda e: e.activation(out=pT, in_=tf, func=AF.Exp, bias=keybias), reads=[("tf", ti)] + list(kb_reads),
                  writes=[("pT", pi)])
        ones = self.ones_bf[0:nk, :]
        rv = list(reads_v)

        def pv():
            self.mm(self.ps[ao][:, 0:nq], ("ps", ao), v_lhsT, pT, first, last, [("pT", pi)] + rv)
            self.mm(self.ps[as_][:, 0:nq], ("ps", as_), ones, pT, first, last, [("pT", pi), "ones_bf"])
            if post is not None:
                post(pT, ("pT", pi))
        self.defer(pv)
        return pT, ("pT", pi)

    def sbank(self):
        return (0, 1, 2, 7)[self.rot("S", 4)]

    def atf(self):
        return [self.tf[0], self.tf[1], self.u[3]]

    def apT(self):
        return [self.pT[0], self.pT[1], self.pT[2], self.u[4][:, 0:256].bitcast(BF16), self.u[4][:, 256:512].bitcast(BF16),
                self.u[5][:, 0:256].bitcast(BF16), self.u[5][:, 256:512].bitcast(BF16)]

    def defer(self, fn, la=5):
        self.pend.append(fn)
        while len(self.pend) > la:
            self.pend.pop(0)()

    def after(self, fn):
        self.pend.append(fn)

    def flush(self):
        while self.pend:
            self.pend.pop(0)()

    def acc_banks(self):
        i = self.rot("acc", 2)
        return 3 + i, 5 + i

    def store_ot(self, src_f32_ap, skeys, oi, t0, n, eng="act"):
        P = self.P
        si = self.rot("sbf", 3)
        sbf = self.sbf[si]
        if eng == "act":
            P.act(lambda e: e.copy(out=sbf[:, 0:n], in_=src_f32_ap), reads=list(skeys), writes=[("sbf", si)])
        else:
            P.pool(lambda e: e.tensor_copy(out=sbf[:, 0:n], in_=src_f32_ap), reads=list(skeys), writes=[("sbf", si)])
        P.dma("sp", self.s["ot"][oi, :, t0:t0 + n], sbf[:, 0:n], reads=[("sbf", si)], writes=[("ot", oi, t0)],
              semkey="sbf%d" % si)

    def mixerA(self, l):
        P = self.P
        Uacc = self.Xv(8192, S)
        sacc = self.Xv(10240, S)
        import os
        glist = [int(c) for c in os.environ.get("AGROUPS", "012")]
        for hh in range(2):
            for g in glist:
                h = 2 * g + hh
                qT, qk = self.ld_fm("q", FM_AQ + h)
                kT, kk = self.ld_fm("k", FM_AK + h)
                v, vk = self.ld_v(VT_A + h)
                tab, tk = self.ld_tab("fa", h, L_A, 1, 127, M_A)
                if g < 2:
                    for qb in range(4):
                        ao, as_ = self.acc_banks()
                        if g == 0:
                            chunks = [(qb * 4 - 1 + c) for c in range(5) if qb * 4 - 1 + c >= 0]
                            offs = {kc: qb * 512 - kc * 128 for kc in chunks}
                        else:
                            chunks = [4 * qb + c for c in range(4)]
                            offs = {kc: -(kc - 4 * qb) * 128 for kc in chunks}
                        for i, kc in enumerate(chunks):
                            m0 = offs[kc] + 384
                            self.attn_tile(kT[:, kc * 128:(kc + 1) * 128], qT[:, qb * 512:(qb + 1) * 512], 128, 512,
                                           tab[:, m0:m0 + 512], v[:, kc, :], ao, as_, i == 0, i == len(chunks) - 1,
                                           [qk, kk], [tk], [vk])
                        def accum(g=g, qb=qb, ao=ao, as_=as_):
                            if g == glist[0] and g == 0:
                                uo = Uacc[:, qb * 512:(qb + 1) * 512]
                                so = sacc[:, qb * 512:(qb + 1) * 512]
                                P.act(lambda e: e.copy(out=uo, in_=self.ps[ao][:, :]), reads=[("ps", ao)], writes=["Uacc"])
                                P.act(lambda e: e.copy(out=so, in_=self.ps[as_][:, :]), reads=[("ps", as_)], writes=["sacc"])
                            else:
                                uo = Uacc.rearrange("p (i r) -> p r i", r=4)[:, qb, :]
                                so = sacc.rearrange("p (i r) -> p r i", r=4)[:, qb, :]
                                P.dve(lambda e: e.tensor_tensor(out=uo, in0=uo, in1=self.ps[ao][:, :], op=ALU.add),
                                      reads=[("ps", ao), "Uacc"], writes=["Uacc"])
                                P.dve(lambda e: e.tensor_tensor(out=so, in0=so, in1=self.ps[as_][:, :], op=ALU.add),
                                      reads=[("ps", as_), "sacc"], writes=["sacc"])
                        self.after(accum)
                else:
                    for b4 in range(4):
                        ao, as_ = self.acc_banks()
                        sbk = self.sbank()
                        for i in range(4):
                            r = 4 * b4 + i
                            self.mm(self.ps[sbk][:, i * 128:(i + 1) * 128], ("ps", sbk), kT[:, r * 128:(r + 1) * 128],
                                    qT[:, r * 128:(r + 1) * 128], True, True, [qk, kk])
                        ti = self.rot("tf", 3)
                        tf = self.atf()[ti]
                        for i in range(4):
                            P.dve(lambda e, i=i, tf=tf, sbk=sbk, tab=tab: e.scalar_tensor_tensor(
                                out=tf[:, i * 128:(i + 1) * 128], in0=self.ps[sbk][:, i * 128:(i + 1) * 128], scalar=SCALE,
                                in1=tab[:, 384:512], op0=ALU.mult, op1=ALU.add),
                                reads=[("ps", sbk), tk], writes=[("tf", ti)])
                        pi = self.rot("pT", 7)
                        pT = self.apT()[pi]
                        P.act(lambda e, pT=pT, tf=tf: e.activation(out=pT[:, :], in_=tf[:, :], func=AF.Exp),
                              reads=[("tf", ti)], writes=[("pT", pi)])
                        def pv2(b4=b4, ao=ao, as_=as_, pT=pT, pi=pi, v=v, vk=vk):
                            for i in range(4):
                                r = 4 * b4 + i
                                self.mm(self.ps[ao][:, i * 128:(i + 1) * 128], ("ps", ao), v[:, r, :], pT[:, i * 128:(i + 1) * 128],
                                        True, True, [("pT", pi), vk])
                                self.mm(self.ps[as_][:, i * 128:(i + 1) * 128], ("ps", as_), self.ones_bf[:, :],
                                        pT[:, i * 128:(i + 1) * 128], True, True, [("pT", pi), "ones_bf"])
                            uo = Uacc.rearrange("p (i r) -> p r i", r=16)[:, 4 * b4:4 * b4 + 4, :]
                            so = sacc.rearrange("p (i r) -> p r i", r=16)[:, 4 * b4:4 * b4 + 4, :]
                            pso = self.ps[ao][:, :].rearrange("p (r i) -> p r i", r=4)
                            pss = self.ps[as_][:, :].rearrange("p (r i) -> p r i", r=4)
                            P.dve(lambda e: e.tensor_tensor(out=uo, in0=uo, in1=pso, op=ALU.add),
                                  reads=[("ps", ao), "Uacc"], writes=["Uacc"])
                            P.dve(lambda e: e.tensor_tensor(out=so, in0=so, in1=pss, op=ALU.add),
                                  reads=[("ps", as_), "sacc"], writes=["sacc"])
                        self.defer(pv2)
            self.flush()
            for qb in range(4):
                sl = slice(qb * 512, (qb + 1) * 512)
                P.dve(lambda e, sl=sl: e.reciprocal(out=sacc[:, sl], in_=sacc[:, sl]), reads=["sacc"], writes=["sacc"])
                P.dve(lambda e, sl=sl: e.tensor_tensor(out=Uacc[:, sl], in0=Uacc[:, sl], in1=sacc[:, sl], op=ALU.mult),
                      reads=["sacc", "Uacc"], writes=["Uacc"])
                self.store_ot(Uacc[:, sl], ["Uacc"], hh, qb * 512, 512)

    def gate_bcast(self, row, qb):
        gsig = self.Xv(12288, S, 12)
        gbk = self.sbank()
        self.mm(self.ps[gbk][:, 0:512], ("ps", gbk), self.sel12[:, row, :], gsig[:, qb * 512:(qb + 1) * 512], True, True,
                ["sel12", "gsig"])
        return gbk

    def finish_branch(self, ao, as_, br, h, qb, first):
        P = self.P
        oB = self.Xv(0, 4 * S).rearrange("p (h t) -> p h t", h=4)
        ri = self.rot("rs", 2)
        rs = self.rs[ri]
        P.act(lambda e: e.activation(out=rs[:, :], in_=self.ps[as_][:, :], func=AF.Ln, bias=self.tiny[:, 0:1]),
              reads=[("ps", as_), "tiny"], writes=[("rs", ri)])
        P.act(lambda e: e.activation(out=rs[:, :], in_=rs[:, :], func=AF.Exp, scale=-1.0), reads=[("rs", ri)],
              writes=[("rs", ri)])
        gbk = self.gate_bcast(br * 4 + h, qb)
        P.dve(lambda e: e.tensor_tensor(out=rs[:, :], in0=rs[:, :], in1=self.ps[gbk][:, :], op=ALU.mult),
              reads=[("rs", ri), ("ps", gbk)], writes=[("rs", ri)])
        dst = oB[:, h, qb * 512:(qb + 1) * 512]
        if first:
            P.dve(lambda e: e.tensor_tensor(out=dst, in0=self.ps[ao][:, :], in1=rs[:, :], op=ALU.mult),
                  reads=[("ps", ao), ("rs", ri)], writes=[("oB", h, qb)])
        else:
            P.dve(lambda e: e.tensor_tensor(out=rs[:, :], in0=self.ps[ao][:, :], in1=rs[:, :], op=ALU.mult),
                  reads=[("ps", ao), ("rs", ri)], writes=[("rs", ri)])
            P.pool(lambda e: e.tensor_tensor(out=dst, in0=dst, in1=rs[:, :], op=ALU.add),
                   reads=[("rs", ri), ("oB", h, qb)], writes=[("oB", h, qb)])

    def mixerB(self, l):
        P = self.P
        imp = self.imp
        qs = []
        for h in range(4):
            qT, qk = self.ld_fm("q", FM_BQ + h)
            tab, tk = self.ld_tab("fc", h, L_C, 16, 2016, M_C)
            for qb in range(4):
                ao, as_ = self.acc_banks()
                post = None
                if qb >= 2:
                    def post(pT, pk, h=h, qb=qb):
                        bi = self.sbank()
                        for sub in range(4):
                            self.mm(self.ps[bi][:, sub * 33:(sub + 1) * 33], ("ps", bi), pT[:, sub * 128:(sub + 1) * 128],
                                    self.ovx[0:N_CMP, :], True, True, [pk, "ovx"])
                        sm = self.small
                        for sub in range(4):
                            qc = qb * 4 + sub
                            P.dve(lambda e, sub=sub: e.tensor_scalar(out=sm[:, 16:17], in0=self.ps[bi][:, sub * 33 + 32:sub * 33 + 33],
                                                                     scalar1=1e-30, scalar2=None, op0=ALU.add),
                                  reads=[("ps", bi)], writes=["sm16"])
                            P.dve(lambda e: e.reciprocal(out=sm[:, 16:17], in_=sm[:, 16:17]), reads=["sm16"], writes=["sm16"])
                            if h == 0:
                                P.dve(lambda e, sub=sub, qc=qc: e.tensor_scalar(
                                    out=imp[:, qc, :], in0=self.ps[bi][:, sub * 33:sub * 33 + 32], scalar1=sm[:, 16:17],
                                    scalar2=None, op0=ALU.mult), reads=[("ps", bi), "sm16"], writes=[("imp", qc)])
                            else:
                                P.dve(lambda e, sub=sub, qc=qc: e.scalar_tensor_tensor(
                                    out=imp[:, qc, :], in0=self.ps[bi][:, sub * 33:sub * 33 + 32], scalar=sm[:, 16:17],
                                    in1=imp[:, qc, :], op0=ALU.mult, op1=ALU.add),
                                    reads=[("ps", bi), "sm16", ("imp", qc)], writes=[("imp", qc)])
                self.attn_tile(self.kcT[:, 0:N_CMP], qT[:, qb * 512:(qb + 1) * 512], N_CMP, 512,
                               tab[0:N_CMP, qb * 512:(qb + 1) * 512], self.vc[0:N_CMP, :], ao, as_, True, True,
                               [qk, "kcT"], [tk], ["vc"], post=post)
                self.after(lambda ao=ao, as_=as_, h=h, qb=qb: self.finish_branch(ao, as_, 0, h, qb, True))
        self.flush()
        sm = self.small
        for qc in range(8, 16):
            sc = sm[:, 20:52]
            P.dve(lambda e, qc=qc: e.tensor_tensor(out=sc, in0=imp[:, qc, :], in1=self.selmul[:, qc, :], op=ALU.mult),
                  reads=[("imp", qc), "selmul"], writes=["sc"])
            P.dve(lambda e, qc=qc: e.tensor_tensor(out=sc, in0=sc, in1=self.seladd[:, qc, :], op=ALU.add),
                  reads=["sc", "seladd"], writes=["sc"])
            wk = self.u[0][:, 0:32]
            m8 = self.u[0][:, 32:40]
            P.dve(lambda e: e.max(out=m8, in_=sc), reads=["sc"], writes=["m8"])
            P.dve(lambda e: e.match_replace(out=wk, in_to_replace=m8, in_values=sc, imm_value=-3.0e38),
                  reads=["sc", "m8"], writes=["wk"])
            P.dve(lambda e: e.max(out=m8, in_=wk), reads=["wk", "m8"], writes=["m8"])
            selb = self.u[0][:, 64:96]
            P.dve(lambda e: e.tensor_scalar(out=selb, in0=sc, scalar1=m8[:, 7:8], scalar2=None, op0=ALU.is_ge),
                  reads=["sc", "m8"], writes=["selb"])
            P.dve(lambda e: e.tensor_scalar(out=selb, in0=selb, scalar1=-1.0, scalar2=-NEG, op0=ALU.add, op1=ALU.mult),
                  reads=["selb"], writes=["selb"])
            b = self.sbank()
            self.tr(self.ps[b][0:32, 0:128], ("ps", b), selb, 128, ["selb"])
            P.act(lambda e, b=b, qc=qc: e.copy(out=self.selbT[:, qc * 128:(qc + 1) * 128], in_=self.ps[b][0:32, 0:128]),
                  reads=[("ps", b)], writes=["selbT"])
        for br, (fmk, vth, tname, L, M) in ((1, (FM_BKS, VT_BS, "fs", L_S, M_S)), (2, (FM_BKW, VT_BW, "fw", L_W, M_W))):
            kT, kk = self.ld_fm("k", fmk)
            v, vk = self.ld_v(vth)
            for h in range(4):
                qT, qk = self.ld_fm("q", FM_BQ + h)
                tab, tk = self.ld_tab(tname, h, L, 1, 127, M)
                for qb in range(4):
                    ao, as_ = self.acc_banks()
                    if br == 1:
                        chunks = list(range(0, 4 * (qb + 1)))
                    else:
                        chunks = list(range(max(0, 4 * qb - 4), 4 * qb + 4))
                    for i, kc in enumerate(chunks):
                        m0 = qb * 512 - kc * 128 + 384
                        sel = None
                        if br == 1 and qb >= 2:
                            sel = (self.eall[:, kc * 128:(kc + 1) * 128], self.selbT[:, qb * 512:(qb + 1) * 512],
                                   ["eall", "selbT"])
                        self.attn_tile(kT[:, kc * 128:(kc + 1) * 128], qT[:, qb * 512:(qb + 1) * 512], 128, 512,
                                       tab[:, m0:m0 + 512], v[:, kc, :], ao, as_, i == 0, i == len(chunks) - 1,
                                       [qk, kk], [tk], [vk], sel=sel)
                    self.after(lambda ao=ao, as_=as_, br=br, h=h, qb=qb: self.finish_branch(ao, as_, br, h, qb, False))
        self.flush()
        oB = self.Xv(0, 4 * S).rearrange("p (h t) -> p h t", h=4)
        for h in range(4):
            for qb in range(4):
                self.store_ot(oB[:, h, qb * 512:(qb + 1) * 512], [("oB", h, qb)], 2 + h, qb * 512, 512, eng="pool")

    def mixerC(self, l):
        P = self.P
        cT = self.Xv(14336, S, 6)
        cqbs = [self.Xv(8192, 512), self.Xv(8704, 512)]
        for h in range(6):
            qT, qk = self.ld_fm("q", FM_CQ + h)
            kT, kk = self.ld_fm("k", FM_CK + h)
            v, vk = self.ld_v(VT_C + h)
            for qb in range(4):
                ci = self.rot("cqb", 2)
                cqb = cqbs[ci]
                cbk = self.sbank()
                self.mm(self.ps[cbk][:, 0:512], ("ps", cbk), self.sel12[0:6, h, :], cT[:, qb * 512:(qb + 1) * 512], True, True,
                        ["sel12", "cT"])
                P.act(lambda e, cbk=cbk, cqb=cqb: e.copy(out=cqb, in_=self.ps[cbk][:, :]), reads=[("ps", cbk)], writes=[("cqb", ci)])
                ao, as_ = self.acc_banks()
                nch = 4 * (qb + 1)
                for kc in range(nch):
                    extra = None
                    if kc >= 4 * qb:
                        m0 = qb * 512 - kc * 128 + 384
                        extra = (self.cmask[:, m0:m0 + 512], ["cmask"])
                    self.attn_tile(kT[:, kc * 128:(kc + 1) * 128], qT[:, qb * 512:(qb + 1) * 512], 128, 512, cqb,
                                   v[:, kc, :], ao, as_, kc == 0, kc == nch - 1, [qk, kk], [("cqb", ci)], [vk],
                                   keybias=self.negck[:, kc, h:h + 1], kb_reads=["negck"], extra=extra)
                def fin(h=h, qb=qb, ao=ao, as_=as_):
                    ri = self.rot("rs", 2)
                    rs = self.rs[ri]
                    P.act(lambda e: e.activation(out=rs[:, :], in_=self.ps[as_][:, :], func=AF.Ln), reads=[("ps", as_)],
                          writes=[("rs", ri)])
                    P.act(lambda e: e.activation(out=rs[:, :], in_=rs[:, :], func=AF.Exp, scale=-1.0), reads=[("rs", ri)],
                          writes=[("rs", ri)])
                    P.dve(lambda e: e.tensor_tensor(out=rs[:, :], in0=self.ps[ao][:, :], in1=rs[:, :], op=ALU.mult),
                          reads=[("ps", ao), ("rs", ri)], writes=[("rs", ri)])
                    self.store_ot(rs[:, :], [("rs", ri)], 6 + h, qb * 512, 512)
                self.after(fin)
        self.flush()

    def ln_stats(self, rbuf, rkeys, dc, n):
        P = self.P
        bs, bq = 5, 6
        ui = self.rot("u", 6)
        u = self.u[ui]
        P.act(lambda e: e.activation(out=u[:, 0:n], in_=rbuf[:, dc, :], func=AF.Square), reads=[rkeys[dc]], writes=[("u", ui)])

        def pe_part():
            self.mm(self.ps[bs][:, 0:n], ("ps", bs), self.ones_f[:, :], rbuf[:, dc, :], dc == 0, dc == 15, ["ones_f", rkeys[dc]])
            self.mm(self.ps[bq][:, 0:n], ("ps", bq), self.ones_f[:, :], u[:, 0:n], dc == 0, dc == 15, ["ones_f", ("u", ui)])
        return pe_part

    def ln_scale(self, n):
        P = self.P
        bs, bq = 5, 6
        mean = self.tf[0][:, 0:n]
        rstd = self.tf[1][:, 0:n]
        P.dve(lambda e: e.tensor_scalar(out=mean, in0=self.ps[bs][:, 0:n], scalar1=1.0 / D, scalar2=None, op0=ALU.mult),
              reads=[("ps", bs)], writes=[("tf", 0)])
        P.dve(lambda e: e.tensor_tensor(out=rstd, in0=mean, in1=mean, op=ALU.mult), reads=[("tf", 0)], writes=[("tf", 1)])
        P.dve(lambda e: e.scalar_tensor_tensor(out=rstd, in0=self.ps[bq][:, 0:n], scalar=1.0 / D, in1=rstd, op0=ALU.mult,
                                               op1=ALU.subtract), reads=[("ps", bq), ("tf", 1)], writes=[("tf", 1)])
        P.dve(lambda e: e.tensor_scalar(out=rstd, in0=rstd, scalar1=LN_EPS, scalar2=None, op0=ALU.add),
              reads=[("tf", 1)], writes=[("tf", 1)])
        P.act(lambda e: e.activation(out=rstd, in_=rstd, func=AF.Ln), reads=[("tf", 1)], writes=[("tf", 1)])
        P.act(lambda e: e.activation(out=rstd, in_=rstd, func=AF.Exp, scale=-0.5), reads=[("tf", 1)], writes=[("tf", 1)])
        return mean, rstd

    def ln_norm(self, rbuf, rkeys, dc, gcol, bcol, n, mean, rstd, xbt=None):
        P = self.P
        ui = self.rot("u", 6)
        u = self.u[ui]
        P.dve(lambda e: e.tensor_tensor(out=u[:, 0:n], in0=rbuf[:, dc, :], in1=mean, op=ALU.subtract),
              reads=[rkeys[dc], ("tf", 0)], writes=[("u", ui)])
        P.dve(lambda e: e.tensor_tensor(out=u[:, 0:n], in0=u[:, 0:n], in1=rstd, op=ALU.mult),
              reads=[("u", ui), ("tf", 1)], writes=[("u", ui)])
        P.act(lambda e: e.activation(out=rbuf[:, dc, :], in_=u[:, 0:n], func=AF.Identity,
                                     bias=self.cols[:, bcol + dc:bcol + dc + 1], scale=self.cols[:, gcol + dc:gcol + dc + 1]),
              reads=[("u", ui), "cols"], writes=[rkeys[dc]])
        if xbt is not None:
            P.act(lambda e: e.activation(out=xbt[:, dc, :], in_=u[:, 0:n], func=AF.Identity,
                                         bias=self.cols[:, bcol + dc:bcol + dc + 1], scale=self.cols[:, gcol + dc:gcol + dc + 1]),
                  reads=[("u", ui), "cols"], writes=[("xbt", dc)])

    def ld_rbuf(self, rbuf, rkeys, src, ts, rd):
        for q4 in range(4):
            self.ld_rbuf_chunk(rbuf, rkeys, src, ts, rd, q4)

    def ld_rbuf_chunk(self, rbuf, rkeys, src, ts, rd, q4):
        self.P.dma("sp", rbuf[:, 4 * q4:4 * q4 + 4, :], src[4 * q4:4 * q4 + 4, :, ts].rearrange("k p t -> p k t"),
                   reads=rd, writes=rkeys[4 * q4:4 * q4 + 4], semkey="rbufld%d" % q4)

    def phaseD1(self, l):
        P, d = self.P, self.d
        oT = self.Zb(0, 12 * S).rearrange("p (k t) -> p k t", k=12)
        mixT = self.xT()
        P.dma("sp", oT, self.s["ot"].rearrange("k p t -> p k t"),
              reads=[("ot", oi, t0) for oi in range(12) for t0 in range(0, S, 512)], writes=["oT"], semkey="otld")
        gts = self.s["gt"].rearrange("(m c) p t -> c p m t", m=3)
        for dcp in range(8):
            si = self.rot("wb", 2)
            wv = self.wb[si][:, 0:12 * 256].rearrange("p (k n) -> p k n", k=12)
            cs = slice(dcp * 256, (dcp + 1) * 256)
            P.dma("pool", wv[:, 0:2, :], d["w_pa"][l, :, cs].rearrange("(k p) n -> p k n", p=128), writes=[("wb", si)],
                  semkey="wb%d" % si)
            P.dma("pool", wv[:, 2:6, :], d["w_pb"][l, :, cs].rearrange("(k p) n -> p k n", p=128), writes=[("wb", si)],
                  semkey="wb%d" % si, join=True)
            P.dma("pool", wv[:, 6:12, :], d["w_pc"][l, :, cs].rearrange("(k p) n -> p k n", p=128), writes=[("wb", si)],
                  semkey="wb%d" % si, join=True)
            wkeys = [("wb", si)]
            for j in range(2):
                dc = 2 * dcp + j
                for tb in range(4):
                    gi = self.rot("gbuf", 2)
                    gb = self.gbuf[gi][:, :].rearrange("p (m t) -> p m t", m=3)
                    P.dma("sp", gb, gts[dc, :, :, tb * 512:(tb + 1) * 512],
                          reads=[("gt", m * 16 + dc, tb) for m in range(3)], writes=[("gbuf", gi)], semkey="gbuf%d" % gi)
                    banks = []
                    for m, (k0, k1) in enumerate(((0, 2), (2, 6), (6, 12))):
                        b = self.rot("ps8", 8)
                        banks.append(b)
                        for kc in range(k0, k1):
                            self.mm(self.ps[b][:, :], ("ps", b), wv[:, kc, j * 128:(j + 1) * 128],
                                    oT[:, kc, tb * 512:(tb + 1) * 512], kc == k0, kc == k1 - 1, wkeys + ["oT"])
                    ts = []
                    for m in range(3):
                        ui = self.rot("u", 6)
                        u = self.u[ui]
                        ts.append((u, ui))
                        P.dve(lambda e, u=u, m=m, b=banks[m], gb=gb: e.tensor_tensor(out=u[:, :], in0=self.ps[b][:, :],
                                                                                     in1=gb[:, m, :], op=ALU.mult),
                              reads=[("ps", banks[m]), ("gbuf", gi)], writes=[("u", ui)])
                    (u0, i0), (u1, i1), (u2, i2) = ts
                    P.pool(lambda e, u0=u0, u1=u1: e.tensor_tensor(out=u0[:, :], in0=u0[:, :], in1=u1[:, :], op=ALU.add),
                           reads=[("u", i0), ("u", i1)], writes=[("u", i0)])
                    P.dve(lambda e, u0=u0, u2=u2, dc=dc, tb=tb: e.tensor_tensor(out=mixT[:, dc, tb * 512:(tb + 1) * 512],
                                                                                in0=u0[:, :], in1=u2[:, :], op=ALU.add),
                          reads=[("u", i0), ("u", i2)], writes=[("mixT", tb)])

    def phaseD2(self, l):
        P, d = self.P, self.d
        mixT = self.xT()
        rbuf = self.Zf(0, 8192).rearrange("p (k t) -> p k t", k=16)
        xbt = self.Zb(8192, 8192).rearrange("p (k t) -> p k t", k=16)
        rkeys = [("rbuf", dc) for dc in range(16)]
        xres_r = [("xres", i) for i in range(4)] + [("xres", i, q) for i in range(4) for q in range(4)]
        self.ld_rbuf(rbuf, rkeys, self.s["xres"], slice(0, 512), xres_r)
        for tb in range(4):
            ts = slice(tb * 512, (tb + 1) * 512)
            pend = []
            for dcp in range(8):
                wv, wkey = self.wload(d["w_out"][l, :, dcp * 256:(dcp + 1) * 256], 16, 256, "wout")
                for j in range(2):
                    dc = 2 * dcp + j
                    b = self.rot("ps8", 4)
                    for kc in range(16):
                        self.mm(self.ps[b][:, :], ("ps", b), wv[:, kc, j * 128:(j + 1) * 128], mixT[:, kc, ts],
                                kc == 0, kc == 15, [wkey, ("mixT", tb)])
                    P.dve(lambda e, dc=dc, b=b: e.scalar_tensor_tensor(out=rbuf[:, dc, :], in0=rbuf[:, dc, :], scalar=ALPHA,
                                                                       in1=self.ps[b][:, :], op0=ALU.mult, op1=ALU.add),
                          reads=[("ps", b), ("rbuf", dc)], writes=[("rbuf", dc)])
                    pend.append(self.ln_stats(rbuf, rkeys, dc, 512))
                    if len(pend) > 2:
                        pend.pop(0)()
            while pend:
                pend.pop(0)()
            mean, rstd = self.ln_scale(512)
            nts = slice((tb + 1) * 512, (tb + 2) * 512)
            for q4 in range(4):
                for dc in range(4 * q4, 4 * q4 + 4):
                    self.ln_norm(rbuf, rkeys, dc, 416, 432, 512, mean, rstd, xbt=xbt)
                P.dma("sp", self.s["x1f"][4 * q4:4 * q4 + 4, :, ts].rearrange("k p t -> p k t"), rbuf[:, 4 * q4:4 * q4 + 4, :],
                      reads=rkeys[4 * q4:4 * q4 + 4], writes=[("x1f", tb, q4)], semkey="rbufst%d" % q4)
                P.dma("sp", self.s["x1b"][4 * q4:4 * q4 + 4, :, ts].rearrange("k p t -> p k t"), xbt[:, 4 * q4:4 * q4 + 4, :],
                      reads=[("xbt", dc) for dc in range(4 * q4, 4 * q4 + 4)], writes=[("x1b", tb, q4)], semkey="xbtst%d" % q4)
                if tb < 3 and q4 >= 1:
                    self.ld_rbuf_chunk(rbuf, rkeys, self.s["xres"], nts, xres_r, q4 - 1)
            if tb < 3:
                self.ld_rbuf_chunk(rbuf, rkeys, self.s["xres"], nts, xres_r, 3)

    def phaseE1(self, l):
        P, d = self.P, self.d
        xT = self.xT()
        P.dma("sp", xT, self.s["x1b"].rearrange("k p t -> p k t"), reads=[("x1b", tb, q4) for tb in range(4) for q4 in range(4)], writes=["x1T"],
              semkey="xTld")
        cols = self.cols
        hr = self.hraw
        for fc in range(NFC):
            si = self.rot("wb", 2)
            wv = self.wb[si][:, 0:4096].rearrange("p (k a n) -> p k a n", k=16, a=2)
            P.dma("pool", wv[:, :, 0, :], d["w_up"][l, :, fc * 128:(fc + 1) * 128].rearrange("(k p) n -> p k n", p=128),
                  writes=[("wb", si)], semkey="wb%d" % si)
            P.dma("pool", wv[:, :, 1, :],
                  d["w_up"][l, :, D_FF + fc * 128:D_FF + (fc + 1) * 128].rearrange("(k p) n -> p k n", p=128),
                  writes=[("wb", si)], semkey="wb%d" % si, join=True)
            wkeys = [("wb", si)]
            for tb in range(4):
                ts = slice(tb * 512, (tb + 1) * 512)
                us = []
                for ab in range(2):
                    b = self.rot("ps8", 8)
                    for kc in range(16):
                        self.mm(self.ps[b][:, :], ("ps", b), wv[:, kc, ab, :], xT[:, kc, ts], kc == 0, kc == 15,
                                wkeys + ["x1T"])
                    h = hr[ab]
                    if tb == 0:
                        P.dve(lambda e, h=h: e.memset(h[:, 0:2], 0.0), reads=[("hraw", ab)], writes=[("hraw", ab)])
                    else:
                        P.act(lambda e, h=h: e.copy(out=h[:, 0:2], in_=h[:, 512:514]), reads=[("hraw", ab)],
                              writes=[("hraw", ab)])
                    P.act(lambda e, h=h, b=b: e.copy(out=h[:, 2:514], in_=self.ps[b][:, :]), reads=[("ps", b), ("hraw", ab)],
                          writes=[("hraw", ab)])
                    ui = self.rot("u", 6)
                    u = self.u[ui]
                    us.append((u, ui))
                    f = fc + 44 * ab
                    w0 = cols[:, 64 + f:65 + f]
                    w1 = cols[:, 64 + 88 + f:65 + 88 + f]
                    w2 = cols[:, 64 + 176 + f:65 + 176 + f]
                    cb = cols[:, 328 + f:329 + f]
                    P.act(lambda e, u=u, b=b, w2=w2, cb=cb: e.activation(out=u[:, :], in_=self.ps[b][:, :], func=AF.Identity,
                                                                         bias=cb, scale=w2),
                          reads=[("ps", b), "cols"], writes=[("u", ui)])
                    P.dve(lambda e, u=u, h=h, w1=w1: e.scalar_tensor_tensor(out=u[:, :], in0=h[:, 1:513], scalar=w1, in1=u[:, :],
                                                                           op0=ALU.mult, op1=ALU.add),
                          reads=[("hraw", ab), ("u", ui), "cols"], writes=[("u", ui)])
                    P.dve(lambda e, u=u, h=h, w0=w0: e.scalar_tensor_tensor(out=u[:, :], in0=h[:, 0:512], scalar=w0, in1=u[:, :],
                                                                           op0=ALU.mult, op1=ALU.add),
                          reads=[("hraw", ab), ("u", ui), "cols"], writes=[("u", ui)])
                (ua, ia), (ub, ib) = us
                P.act(lambda e, ua=ua: e.activation(out=ua[:, :], in_=ua[:, :], func=AF.Gelu), reads=[("u", ia)],
                      writes=[("u", ia)])
                si2 = self.rot("sbf", 3)
                sbf = self.sbf[si2]
                P.dve(lambda e, ua=ua, ub=ub, sbf=sbf: e.tensor_tensor(out=sbf[:, :], in0=ua[:, :], in1=ub[:, :], op=ALU.mult),
                      reads=[("u", ia), ("u", ib)], writes=[("sbf", si2)])
                P.dma("sp", self.s["zt"][fc, :, ts], sbf[:, :], reads=[("sbf", si2)], writes=[("zt", fc, tb)],
                      semkey="sbf%d" % si2)

    def phaseE2(self, l, last):
        P, d = self.P, self.d
        zbuf = self.Zb(0, NFC * 512).rearrange("p (k t) -> p k t", k=NFC)
        rbuf = self.Xv(0, 8192).rearrange("p (k t) -> p k t", k=16)
        xbt = self.Xv(8192, 4096).bitcast(BF16).rearrange("p (k t) -> p k t", k=16)
        orow = [self.Xv(8192, 2048), self.Xv(10240, 2048)]
        rkeys = [("rbuf", dc) for dc in range(16)]
        if not last:
            self.load_spw(l + 1)
        self.ld_zbuf(zbuf, 0)
        self.ld_rbuf(rbuf, rkeys, self.s["x1f"], slice(0, 512), [("x1f", 0, q4) for q4 in range(4)])
        for tb in range(4):
            ts = slice(tb * 512, (tb + 1) * 512)
            nts = slice((tb + 1) * 512, (tb + 2) * 512)
            nrd = [("x1f", tb + 1, q4) for q4 in range(4)]
            pend = []
            for dc in range(16):
                if tb == 0:
                    wv, wkey = self.wload(d["w_down"][l, :, dc * 128:(dc + 1) * 128], NFC, 128, "wdown")
                    si = wkey[1]
                    P.dma("sp", self.s["wdc"][dc], self.wb[si][:, 0:NFC * 128], reads=[wkey], writes=[("wdc", dc)],
                          semkey="wdcst%d" % si)
                else:
                    si = self.rot("wb", 2)
                    wkey = ("wb", si)
                    wv = self.wb[si][:, 0:NFC * 128].rearrange("p (k n) -> p k n", k=NFC)
                    P.dma("pool", self.wb[si][:, 0:NFC * 128], self.s["wdc"][dc], reads=[("wdc", dc)], writes=[wkey],
                          semkey="wb%d" % si)
                b = self.rot("ps8", 4)
                for kc in range(NFC):
                    self.mm(self.ps[b][:, :], ("ps", b), wv[:, kc, :], zbuf[:, kc, :], kc == 0, kc == NFC - 1,
                            [wkey, ("zbuf", kc // 11)])
                P.dve(lambda e, dc=dc, b=b: e.scalar_tensor_tensor(out=rbuf[:, dc, :], in0=rbuf[:, dc, :], scalar=ALPHA,
                                                                   in1=self.ps[b][:, :], op0=ALU.mult, op1=ALU.add),
                      reads=[("ps", b), ("rbuf", dc)], writes=[("rbuf", dc)])
                pend.append(self.ln_stats(rbuf, rkeys, dc, 512))
                if len(pend) > 1:
                    pend.pop(0)()
            while pend:
                pend.pop(0)()
            if tb < 3:
                self.ld_zbuf(zbuf, tb + 1)
            mean, rstd = self.ln_scale(512)
            if last:
                for dc in range(16):
                    self.ln_norm(rbuf, rkeys, dc, 448, 464, 512, mean, rstd)
                for c in range(4):
                    oi = self.rot("orow", 2)
                    orw = orow[oi]
                    for dq in range(4):
                        b = self.rot("ps8", 4)
                        for j in range(4):
                            dc = 4 * dq + j
                            self.tr(self.ps[b][:, j * 128:(j + 1) * 128], ("ps", b), rbuf[:, dc, c * 128:(c + 1) * 128], 128,
                                    [("rbuf", dc)])
                        if dq % 2 == 0:
                            P.act(lambda e, orw=orw, b=b, dq=dq: e.copy(out=orw[:, dq * 512:(dq + 1) * 512], in_=self.ps[b][:, :]),
                                  reads=[("ps", b)], writes=[("orow", oi, dq)])
                        else:
                            P.dve(lambda e, orw=orw, b=b, dq=dq: e.tensor_copy(out=orw[:, dq * 512:(dq + 1) * 512], in_=self.ps[b][:, :]),
                                  reads=[("ps", b)], writes=[("orow", oi, dq)])
                    t0 = tb * 512 + c * 128
                    o = P.dma("sp", d["out"][t0:t0 + 128, :], orw, reads=[("orow", oi, dq) for dq in range(4)],
                              writes=[("out", t0)], semkey="orow%d" % oi)
                    self.outs.append(o)
                if tb < 3:
                    self.ld_rbuf(rbuf, rkeys, self.s["x1f"], nts, nrd)
            else:
                for q4 in range(4):
                    for dc in range(4 * q4, 4 * q4 + 4):
                        self.ln_norm(rbuf, rkeys, dc, 448, 464, 512, mean, rstd, xbt=xbt)
                    P.dma("sp", self.s["xres"][4 * q4:4 * q4 + 4, :, ts].rearrange("k p t -> p k t"), rbuf[:, 4 * q4:4 * q4 + 4, :],
                          reads=rkeys[4 * q4:4 * q4 + 4], writes=[("xres", tb, q4)], semkey="rbufst%d" % q4)
                    P.dma("sp", self.s["xb"][4 * q4:4 * q4 + 4, :, ts].rearrange("k p t -> p k t"), xbt[:, 4 * q4:4 * q4 + 4, :],
                          reads=[("xbt", dc) for dc in range(4 * q4, 4 * q4 + 4)], writes=[("xb", tb, q4)], semkey="xbtst%d" % q4)
                self.special_proj(l + 1, rbuf, rkeys, tb * 512, 512)
                if tb < 3:
                    self.ld_rbuf(rbuf, rkeys, self.s["x1f"], nts, nrd)

    def ld_zbuf(self, zbuf, tb):
        ts = slice(tb * 512, (tb + 1) * 512)
        for j in range(4):
            self.P.dma("sp", zbuf[:, 11 * j:11 * j + 11, :], self.s["zt"][11 * j:11 * j + 11, :, ts].rearrange("k p t -> p k t"),
                       reads=[("zt", fc, tb) for fc in range(11 * j, 11 * j + 11)], writes=[("zbuf", j)], semkey="zbufld%d" % j)

    def load_params(self, l):
        d = self.d
        for j in range(3):
            self.load_cols(64 + 88 * j, d["conv_w"][l, j], 88)
        self.load_cols(328, d["conv_b"][l], 88)
        self.load_cols(416, d["ln1_g"][l], 16)
        self.load_cols(432, d["ln1_b"][l], 16)
        self.load_cols(448, d["ln2_g"][l], 16)
        self.load_cols(464, d["ln2_b"][l], 16)

    def build(self):
        P = self.P
        self.alloc()
        self.gbuf = [self.sb("gbuf%d" % i, [128, 1536], BF16) for i in range(2)]
        upto = getattr(self, "upto", None)
        order = ["setup", "A", "B", "S", "M", "mA", "mB", "mC", "D1", "D2", "E1", "E2"]

        self.marks = []

        def on(name):
            self.marks.append((name, len(P.ops["pe"])))
            return upto is None or order.index(name) <= order.index(upto)
        self.setup()
        P.barrier()
        for li in range(self.n_layers):
            l = self.first_layer + li
            last = li == self.n_layers - 1
            if li == 0:
                if on("A"):
                    self.phaseA(l)
            else:
                P.dma("sp", self.xT(), self.s["xb"].rearrange("k p t -> p k t"), reads=[("xb", tb, q) for tb in range(4) for q in range(4)],
                      writes=[("xT", i) for i in range(8)], semkey="xTld")
            if on("B"):
                self.load_params(l)
                self.phaseB(l)
                P.barrier()
            if on("S"):
                self.phaseS(l)
            if on("M"):
                self.phaseM(l)
                P.barrier()
            if on("mA"):
                self.mixerA(l)
                P.barrier()
            if on("mB"):
                self.mixerB(l)
                P.barrier()
            if on("mC"):
                self.mixerC(l)
                P.barrier()
            if on("D1"):
                self.phaseD1(l)
                P.barrier()
            if on("D2"):
                self.phaseD2(l)
                P.barrier()
            if on("E1"):
                self.phaseE1(l)
                P.barrier()
            if on("E2"):
                self.phaseE2(l, last)
                P.barrier()
        stats = P.emit(final_ops=self.outs)
        self.st.close()
        return self.nc, stats


_PROG_CACHE = {}


def _get_prog(key, **kw):
    if key not in _PROG_CACHE:
        b = Builder(**kw)
        nc, stats = b.build()
        _PROG_CACHE[key] = nc
    return _PROG_CACHE[key]


def _in_maps(x, weights):
    consts = host_consts()
    maps = []
    for b in range(x.shape[0]):
        m = {"x": np.ascontiguousarray(x[b], dtype=np.float32)}
        for k in WEIGHT_SHAPES:
            m[k] = weights[k]
        for k in CONST_SHAPES:
            m["c_" + k] = consts[k]
        maps.append(m)
    return maps


def kernel(**inputs):
    x = np.asarray(inputs["x"], dtype=np.float32)
    weights = {k: np.ascontiguousarray(np.asarray(inputs[k], dtype=np.float32)) for k in WEIGHT_SHAPES}
    nc = _get_prog("full", n_layers=DEPTH, first_layer=0)
    res = run_bass_kernel_spmd(nc, _in_maps(x, weights), core_ids=list(range(8)))
    return np.stack([r["out"] for r in res.results], axis=0).astype(np.float32)
```

```python
import math
from contextlib import ExitStack

import numpy as np
import concourse.bass as bass
import concourse.mybir as mybir
from concourse.bass_utils import run_bass_kernel_spmd

F32 = mybir.dt.float32
BF16 = mybir.dt.bfloat16
AF = mybir.ActivationFunctionType
ALU = mybir.AluOpType

S = 2048
D = 2048
DEPTH = 2
DH = 128
N_IN = 5906
D_FF = 5632
NFC = D_FF // 128
ALPHA = (2 * DEPTH) ** 0.25
LN_EPS = 1e-5
SCALE = DH ** -0.5
NEG = -30000.0
N_CMP = 127

C_AQ, C_AK, C_AV, C_BQ = 0, 768, 1536, 2304
C_BKC, C_BVC, C_BKS, C_BVS, C_BKW, C_BVW = 2816, 2944, 3072, 3200, 3328, 3456
C_BG, C_CQ, C_CK, C_CV, C_CF = 3584, 3596, 4364, 5132, 5900

FM_AQ, FM_AK, FM_BQ, FM_BKC, FM_BVC, FM_BKS, FM_BKW, FM_CQ, FM_CK = 0, 6, 12, 16, 17, 18, 19, 20, 26
VT_A, VT_BS, VT_BW, VT_C = 0, 6, 7, 8

ENGS = ("pe", "act", "dve", "pool", "sp")


class Op:
    __slots__ = ("eng", "fn", "deps", "dma", "semkey", "signal", "needed")

    def __init__(self, eng, fn, dma, semkey):
        self.eng = eng
        self.fn = fn
        self.deps = []
        self.dma = dma
        self.semkey = semkey
        self.signal = None
        self.needed = False


class Prog:
    def __init__(self, nc):
        self.nc = nc
        self.ops = {e: [] for e in ENGS}
        self.last_w = {}
        self.readers = {}
        self.bar = {}

    def barrier(self):
        deps = []
        for e in ENGS:
            for op in reversed(self.ops[e]):
                if not op.dma:
                    deps.append(op)
                    break
        lastd = {}
        for e in ENGS:
            for op in self.ops[e]:
                if op.dma:
                    lastd[op.semkey] = op
        deps.extend(lastd.values())
        for d in deps:
            d.needed = True
        self.bar = {e: list(deps) for e in ENGS}
        self.last_w = {}
        self.readers = {}

    def add(self, eng, fn, reads=(), writes=(), dma=False, semkey=None):
        op = Op(eng, fn, dma, semkey)
        if self.bar.get(eng):
            for d in self.bar.pop(eng):
                if d.eng == eng and not d.dma:
                    continue
                op.deps.append(d)
        deps = []
        for r in reads:
            w = self.last_w.get(r)
            if w is not None:
                deps.append(w)
            if isinstance(r, tuple) and r[0] == "ps":
                rd = self.readers.get(r)
                if rd:
                    deps.extend(o for o in rd.values() if o.eng != eng)
        for wk in writes:
            w = self.last_w.get(wk)
            if w is not None:
                deps.append(w)
            rd = self.readers.get(wk)
            if rd:
                deps.extend(rd.values())
        seen = set()
        for d in deps:
            if id(d) in seen or d is op:
                continue
            seen.add(id(d))
            if d.eng == "pe" and eng == "pe" and not d.dma and not dma:
                continue
            op.deps.append(d)
            d.needed = True
        for wk in writes:
            self.last_w[wk] = op
            self.readers[wk] = {}
        for r in reads:
            rd = self.readers.setdefault(r, {})
            if dma:
                rd[("dma", id(op))] = op
            else:
                rd[eng] = op
        self.ops[eng].append(op)
        return op

    def pe(self, fn, reads=(), writes=()):
        return self.add("pe", fn, reads, writes)

    def act(self, fn, reads=(), writes=()):
        return self.add("act", fn, reads, writes)

    def dve(self, fn, reads=(), writes=()):
        return self.add("dve", fn, reads, writes)

    def pool(self, fn, reads=(), writes=()):
        return self.add("pool", fn, reads, writes)

    def dma(self, queue, out, in_, reads=(), writes=(), semkey=None, join=False, **kw):
        op = self.add(queue, lambda e: e.dma_start(out=out, in_=in_, **kw), reads, writes,
                      dma=True, semkey=semkey)
        if join or (isinstance(semkey, str) and semkey.startswith("grp_")):
            op.deps = [d for d in op.deps if not (d.dma and d.semkey == semkey and d.eng == queue)]
        return op

    def emit(self, final_ops=()):
        nc = self.nc
        semkeys = []
        skset = set()
        for e in ENGS:
            for op in self.ops[e]:
                if op.dma and op.semkey not in skset:
                    skset.add(op.semkey)
                    semkeys.append(op.semkey)
        with ExitStack() as st:
            prog_sem = {e: st.enter_context(nc.semaphore("prog_" + e)) for e in ENGS}
            dma_sem = {k: st.enter_context(nc.semaphore("dma_%d" % i)) for i, k in enumerate(semkeys)}
            cnt = {e: 0 for e in ENGS}
            dcnt = {k: 0 for k in semkeys}
            for e in ENGS:
                for op in self.ops[e]:
                    if op.dma:
                        dcnt[op.semkey] += 16
                        op.signal = (dma_sem[op.semkey], dcnt[op.semkey])
                    elif op.needed:
                        cnt[e] += 1
                        op.signal = (prog_sem[e], cnt[e])
            for e in ENGS:
                for op in self.ops[e]:
                    if op.dma and isinstance(op.semkey, str) and op.semkey.startswith("grp_"):
                        op.signal = (dma_sem[op.semkey], dcnt[op.semkey])
            block = st.enter_context(nc.Block())
            engmap = {"pe": block.tensor, "act": block.scalar, "dve": block.vector,
                      "pool": block.gpsimd, "sp": block.sync}
            stats = {}

            def mk(e):
                def body(eng):
                    seen = {}
                    nw = 0
                    for op in self.ops[e]:
                        for d in op.deps:
                            sem, val = d.signal
                            k = id(sem)
                            if seen.get(k, 0) >= val:
                                continue
                            seen[k] = val
                            eng.wait_ge(sem, val)
                            nw += 1
                        ins = op.fn(eng)
                        if op.signal is not None:
                            ins.then_inc(op.signal[0], 16 if op.dma else 1)
                    if e == "sp":
                        for d in final_ops:
                            sem, val = d.signal
                            eng.wait_ge(sem, val)
                    stats[e] = (len(self.ops[e]), nw)
                return body

            for e in ENGS:
                engmap[e](mk(e))
        return stats


def _t5_bucket(dist):
    n_buckets, max_distance = 32, 2048
    max_exact = n_buckets // 2
    d = np.maximum(np.asarray(dist, np.int32), 0)
    ratio = np.maximum(d, 1).astype(np.float32) / np.float32(max_exact)
    log_ratio = np.log(ratio).astype(np.float32) / np.float32(math.log(max_distance / max_exact))
    large = np.minimum(max_exact + (log_ratio * np.float32(n_buckets - max_exact)).astype(np.int32), n_buckets - 1)
    return np.where(d < max_exact, d, large)


L_A, L_S, L_W, L_C = 1152, 3072, 1536, 4096
M_A, M_S, M_W, M_C = 1024, 2816, 1408, 2048


def _onehot(L, off, dil, lo, hi):
    i = np.arange(L)
    d = i - off
    valid = (d >= lo) & (d <= hi)
    b = _t5_bucket(np.clip(d, 0, None) * dil)
    oh = np.zeros((33, L), np.float32)
    oh[b[valid], i[valid]] = 1.0
    oh[32, i[~valid]] = 1.0
    return oh


_CONSTS = None


def host_consts():
    global _CONSTS
    if _CONSTS is not None:
        return _CONSTS
    c = {}
    c["ident"] = np.eye(128, dtype=np.float32)
    oh_a = np.stack([_onehot(L_A, 511, dil, 0, 128) for dil in (1, 4, 16)])
    c["oh_a"] = np.ascontiguousarray(oh_a.transpose(1, 0, 2))
    c["oh_s"] = _onehot(L_S, 511, 1, 0, 2047)
    c["oh_w"] = _onehot(L_W, 511, 1, 0, 511)
    c["oh_c"] = _onehot(L_C, 2047, 1, 0, 2047)
    p = np.arange(128)[:, None]
    m = np.arange(1024)[None, :]
    c["cmask"] = np.where(m - 384 - p >= 0, 0.0, NEG).astype(np.float32)
    ci = np.arange(N_CMP)[:, None]
    sj = np.arange(32)[None, :]
    ov = ((ci * 16 <= sj * 64 + 63) & (ci * 16 + 31 >= sj * 64)).astype(np.float32)
    c["ovx"] = np.concatenate([ov, np.ones((N_CMP, 1), np.float32)], axis=1)
    c["eall"] = (np.arange(S)[None, :] // 64 == np.arange(32)[:, None]).astype(np.float32)
    pos = np.arange(S)[:, None]
    blk = np.arange(32)[None, :]
    cur = pos // 64
    forced = (blk == 0) | (blk == cur) | (blk == cur - 1)
    causal = blk * 64 <= pos
    selmul = (causal & ~forced).astype(np.float32)
    seladd = np.where(forced, 1e9, np.where(causal, 0.0, -1e30)).astype(np.float32)
    c["selmul"] = np.ascontiguousarray(selmul.reshape(16, 128, 32).transpose(1, 0, 2))
    c["seladd"] = np.ascontiguousarray(seladd.reshape(16, 128, 32).transpose(1, 0, 2))
    sel12 = np.zeros((12, 12, 128), np.float32)
    for h in range(12):
        sel12[h, h, :] = 1.0
    c["sel12"] = sel12
    _CONSTS = c
    return c


CONST_SHAPES = {
    "ident": [128, 128], "oh_a": [33, 3, L_A], "oh_s": [33, L_S], "oh_w": [33, L_W], "oh_c": [33, L_C],
    "cmask": [128, 1024], "ovx": [N_CMP, 33], "eall": [32, S], "selmul": [128, 16, 32],
    "seladd": [128, 16, 32], "sel12": [12, 12, 128],
}

WEIGHT_SHAPES = {
    "rel_bias": [32, 10], "w_in": [2, D, N_IN], "b_f": [2, 6], "b_nsa_gate": [2, 12],
    "cmp_pe": [2, 2, 32, 128], "cmp_w1": [2, 2, 4096, 128], "cmp_b1": [2, 2, 128],
    "cmp_w2": [2, 2, 128, 128], "cmp_b2": [2, 2, 128], "w_gate": [2, D, 3 * D], "b_gate": [2, 3 * D],
    "w_pa": [2, 256, D], "w_pb": [2, 512, D], "w_pc": [2, 768, D], "w_out": [2, D, D],
    "ln1_g": [2, D], "ln1_b": [2, D], "w_up": [2, D, 2 * D_FF], "conv_w": [2, 3, 2 * D_FF],
    "conv_b": [2, 2 * D_FF], "w_down": [2, D_FF, D], "ln2_g": [2, D], "ln2_b": [2, D],
}


class Builder:
    def __init__(self, n_layers=DEPTH, first_layer=0, do_setup=True, dbg=()):
        self.n_layers = n_layers
        self.first_layer = first_layer
        self.dbg = set(dbg)
        self.nc = nc = bass.Bass("TRN2", target_bir_lowering=False)
        self.P = Prog(nc)
        self.st = ExitStack()
        self.dmac = 0
        d = {}
        d["x"] = nc.dram_tensor("x", [S, D], F32, kind="ExternalInput").ap()
        for k, shp in WEIGHT_SHAPES.items():
            d[k] = nc.dram_tensor(k, shp, F32, kind="ExternalInput").ap()
        for k, shp in CONST_SHAPES.items():
            d[k] = nc.dram_tensor("c_" + k, shp, F32, kind="ExternalInput").ap()
        d["out"] = nc.dram_tensor("out", [S, D], F32, kind="ExternalOutput").ap()
        self.d = d
        self.outs = []
        self.s = {}
        self.sh = {}

        def scr(name, shape, dt):
            kind = "ExternalOutput" if name in self.dbg else "Internal"
            h = nc.dram_tensor("s_" + name, shape, dt, kind=kind)
            self.sh[name] = h
            self.s[name] = h.ap()
        scr("xres", [16, 128, S], F32)
        scr("x1f", [16, 128, S], F32)
        scr("xb", [16, 128, S], BF16)
        scr("x1b", [16, 128, S], BF16)
        scr("fm", [32, 128, S], BF16)
        scr("vt", [14, S, 128], BF16)
        scr("gt", [48, 128, S], BF16)
        scr("ot", [12, 128, S], BF16)
        scr("zt", [NFC, 128, S], BF16)
        scr("spec", [2, 12, S], F32)
        scr("wdc", [16, 128, NFC * 128], BF16)
        scr("fa", [6, 128, L_A], F32)
        scr("fs", [4, 128, L_S], F32)
        scr("fw", [4, 128, L_W], F32)
        scr("fc", [4, 128, L_C], F32)

    def sb(self, name, shape, dt):
        return self.st.enter_context(self.nc.sbuf_tensor(name, shape, dt))

    def alloc(self):
        nc = self.nc
        st = self.st
        self.X = self.sb("X", [128, 16384], F32)
        self.Z = self.sb("Z", [128, 12288], F32)
        self.ps = [st.enter_context(nc.psum_tensor("ps%d" % i, [128, 512], F32)) for i in range(8)]
        self.wb = [self.sb("wb%d" % i, [128, 6144], BF16) for i in range(2)]
        self.ident = self.sb("ident", [128, 128], F32)
        self.ones_bf = self.sb("ones_bf", [128, 128], BF16)
        self.ones_f = self.sb("ones_f", [128, 128], F32)
        self.cmask = self.sb("cmask", [128, 1024], F32)
        self.eall = self.sb("eall", [32, S], BF16)
        self.ovx = self.sb("ovx", [128, 33], BF16)
        self.selmul = self.sb("selmul", [128, 16, 32], F32)
        self.seladd = self.sb("seladd", [128, 16, 32], F32)
        self.sel12 = self.sb("sel12", [12, 12, 128], F32)
        self.selbT = self.sb("selbT", [32, S], BF16)
        self.imp = self.sb("imp", [128, 16, 32], F32)
        self.negck = self.sb("negck", [128, 16, 6], F32)
        self.cols = self.sb("cols", [128, 512], F32)
        self.kcT = self.sb("kcT", [128, 128], BF16)
        self.vc = self.sb("vc", [128, 128], BF16)
        self.small = self.sb("small", [128, 64], F32)
        self.spw = self.sb("spw", [128, 16, 18], F32)
        self.tiny = self.sb("tiny", [128, 2], F32)
        self.tf = [self.sb("tf%d" % i, [128, 512], F32) for i in range(2)]
        self.pT = [self.sb("pT%d" % i, [128, 512], BF16) for i in range(3)]
        self.sf = [self.sb("sf%d" % i, [128, 512], F32) for i in range(2)]
        self.sbf = [self.sb("sbf%d" % i, [128, 512], BF16) for i in range(3)]
        self.rs = [self.sb("rs%d" % i, [128, 512], F32) for i in range(2)]
        self.hraw = [self.sb("hraw%d" % i, [128, 514], F32) for i in range(2)]
        self.u = [self.sb("u%d" % i, [128, 512], F32) for i in range(6)]
        self.rr = {}
        self.pend = []

    def rot(self, name, n):
        v = self.rr.get(name, 0)
        self.rr[name] = v + 1
        return v % n

    def xT(self):
        return self.X[:].bitcast(BF16).rearrange("p (k t) -> p k t", k=16)

    def Zf(self, off, n):
        return self.Z[:, off:off + n]

    def Zb(self, off_f32, n_bf):
        return self.Z[:, off_f32:off_f32 + n_bf // 2].bitcast(BF16)

    def mm(self, ps_ap, pskey, lhsT, rhs, start, stop, reads):
        self.P.pe(lambda e: e.matmul(out=ps_ap, lhsT=lhsT, rhs=rhs, start=start, stop=stop),
                  reads=reads, writes=[pskey])

    def tr(self, ps_ap, pskey, in_, n_in_part, reads):
        ident = self.ident[0:n_in_part, 0:n_in_part]
        self.P.pe(lambda e: e.transpose(out=ps_ap, in_=in_, identity=ident),
                  reads=list(reads) + ["ident"], writes=[pskey])

    def load_cols(self, col0, vec_ap, n):
        P = self.P
        stg = self.sf[1]
        P.dma("sp", stg[0:n, 0:128], vec_ap.rearrange("(n p) -> n p", p=128), writes=[("sf", 1)], semkey="sfld")
        b = 7
        self.tr(self.ps[b][:, 0:n], ("ps", b), stg[0:n, 0:128], n, [("sf", 1)])
        cols = self.cols
        P.act(lambda e: e.copy(out=cols[:, col0:col0 + n], in_=self.ps[b][:, 0:n]),
              reads=[("ps", b)], writes=["cols"])

    def setup(self):
        P, d = self.P, self.d
        g = "grp_setup"
        P.dma("sp", self.ident[:], d["ident"], writes=["ident"], semkey=g)
        P.dma("sp", self.cmask[:], d["cmask"], writes=["cmask"], semkey=g)
        P.dma("sp", self.selmul[:], d["selmul"], writes=["selmul"], semkey=g)
        P.dma("sp", self.seladd[:], d["seladd"], writes=["seladd"], semkey=g)
        P.dma("sp", self.sel12[:], d["sel12"], writes=["sel12"], semkey=g)
        P.dma("pool", self.eall[:], d["eall"], writes=["eall"], semkey="grp_setup_p")
        P.dma("pool", self.ovx[0:N_CMP, :], d["ovx"], writes=["ovx"], semkey="grp_setup_p")
        P.dve(lambda e: e.memset(self.ones_bf[:], 1.0), writes=["ones_bf"])
        P.dve(lambda e: e.memset(self.tiny[:], 1e-30), writes=["tiny"])
        P.dve(lambda e: e.memset(self.ones_f[:], 1.0), writes=["ones_f"])
        P.dve(lambda e: e.memset(self.selbT[:], 0.0), writes=["selbT"])
        X = self.X
        tbl = X[0:32, 0:10]
        lhs_all = X[0:33, 16:16 + 1280].rearrange("p (h m) -> p h m", h=10)
        oh_a = X[0:33, 1536:1536 + 3 * L_A].rearrange("p (g l) -> p g l", g=3)
        o1 = 1536 + 3 * L_A
        oh_s = X[0:33, o1:o1 + L_S]
        oh_w = X[0:33, o1 + L_S:o1 + L_S + L_W]
        oh_c = X[0:33, o1 + L_S + L_W:o1 + L_S + L_W + L_C]
        P.dma("sp", tbl, d["rel_bias"], writes=["tbl"], semkey=g)
        P.dma("sp", oh_a, d["oh_a"], writes=["oh"], semkey=g)
        P.dma("sp", oh_s, d["oh_s"], writes=["oh"], semkey=g)
        P.dma("sp", oh_w, d["oh_w"], writes=["oh"], semkey=g)
        P.dma("sp", oh_c, d["oh_c"], writes=["oh"], semkey=g)
        P.dve(lambda e: e.memset(X[32:33, 16:16 + 1280], NEG), writes=["lhs_neg"])
        for h in range(10):
            P.dve(lambda e, h=h: e.tensor_copy(out=lhs_all[0:32, h, :], in_=tbl[:, h:h + 1].to_broadcast([32, 128])),
                  reads=["tbl"], writes=[("lhs", h)])
        jobs = []
        for h in range(6):
            jobs.append((h, oh_a[:, h // 2, :], L_A, self.s["fa"][h]))
        for h in range(4):
            jobs.append((6 + h, oh_s, L_S, self.s["fs"][h]))
            jobs.append((6 + h, oh_w, L_W, self.s["fw"][h]))
            jobs.append((6 + h, oh_c, L_C, self.s["fc"][h]))
        for (h, oh, L, dst) in jobs:
            for c0 in range(0, L, 512):
                n = min(512, L - c0)
                b = self.rot("ps8", 8)
                self.mm(self.ps[b][:, 0:n], ("ps", b), lhs_all[0:33, h, :], oh[:, c0:c0 + n], True, True,
                        [("lhs", h), "lhs_neg", "oh"])
                si = self.rot("sf", 2)
                sf = self.sf[si]
                P.act(lambda e, sf=sf, b=b, n=n: e.copy(out=sf[:, 0:n], in_=self.ps[b][:, 0:n]),
                      reads=[("ps", b)], writes=[("sf", si)])
                P.dma("sp", dst[:, c0:c0 + n], sf[:, 0:n], reads=[("sf", si)], writes=["ftab"], semkey="sf%d" % si)

    def tab_src(self, name, h, L, s, C, M):
        hd = self.sh[name]
        return bass.AP(tensor=hd, offset=h * 128 * L + C, ap=[[L - s, 128], [1, M]])

    def special_proj(self, l, xf, xfreads, t0, n):
        P = self.P
        for (j, nc_, c0) in ((0, 12, 0), (1, 6, 12)):
            b = self.rot("ps8", 8)
            for kc in range(16):
                self.mm(self.ps[b][0:nc_, 0:n], ("ps", b), self.spw[:, kc, c0:c0 + nc_], xf[:, kc, 0:n],
                        kc == 0, kc == 15, [("spw", q, jj) for q in range(4) for jj in range(2)] + xfreads)
            si = self.rot("sf", 2)
            sf = self.sf[si]
            P.act(lambda e, sf=sf, b=b, nc_=nc_: e.copy(out=sf[0:nc_, 0:n], in_=self.ps[b][0:nc_, 0:n]),
                  reads=[("ps", b)], writes=[("sf", si)])
            P.dma("sp", self.s["spec"][j, 0:nc_, t0:t0 + n], sf[0:nc_, 0:n], reads=[("sf", si)],
                  writes=[("spec", j, t0)], semkey="sf%d" % si)

    def load_spw(self, l):
        P, d = self.P, self.d
        for q in range(4):
            ks = slice(q * 512, (q + 1) * 512)
            P.dma("sp", self.spw[:, 4 * q:4 * q + 4, 0:12],
                  d["w_in"][l, ks, C_BG:C_BG + 12].rearrange("(k p) n -> p k n", p=128),
                  writes=[("spw", q, 0)], semkey="grp_spw%d" % l)
            P.dma("sp", self.spw[:, 4 * q:4 * q + 4, 12:18],
                  d["w_in"][l, ks, C_CF:C_CF + 6].rearrange("(k p) n -> p k n", p=128),
                  writes=[("spw", q, 1)], semkey="grp_spw%d" % l)

    def phaseA(self, l):
        P, d = self.P, self.d
        xT = self.xT()
        xins = [self.Zf(0, 4096).rearrange("p (c d) -> p c d", c=2), self.Zf(8192, 4096).rearrange("p (c d) -> p c d", c=2)]
        xf = self.Zf(4096, 4096).rearrange("p (k t) -> p k t", k=16)
        import os
        nospec = os.environ.get("NOSPEC") == "1"
        if not nospec:
            self.load_spw(l)
        for t8 in range(8):
            t0 = t8 * 256
            xin = xins[t8 % 2]
            xkey = ("xin", t8 % 2)
            P.dma("sp", xin, d["x"][t0:t0 + 256, :].rearrange("(c p) d -> p c d", p=128), writes=[xkey], semkey="xin%d" % (t8 % 2))
            for b2 in range(8):
                b = self.rot("ps8", 8)
                for j in range(2):
                    dc = 2 * b2 + j
                    for c in range(2):
                        self.tr(self.ps[b][:, j * 256 + c * 128:j * 256 + (c + 1) * 128], ("ps", b),
                                xin[:, c, dc * 128:(dc + 1) * 128], 128, [xkey])
                psv = self.ps[b][:, 0:512].rearrange("p (j t) -> p j t", j=2)
                P.act(lambda e, psv=psv, b2=b2: e.copy(out=xf[:, 2 * b2:2 * b2 + 2, :], in_=psv),
                      reads=[("ps", b)], writes=[("xf", b2)])
                P.dve(lambda e, psv=psv, b2=b2, t0=t0: e.tensor_copy(out=xT[:, 2 * b2:2 * b2 + 2, t0:t0 + 256], in_=psv),
                      reads=[("ps", b)], writes=[("xT", t8)])
            xfr = [("xf", i) for i in range(8)]
            P.dma("sp", self.s["xres"][:, :, t0:t0 + 256].rearrange("k p t -> p k t"), xf, reads=xfr,
                  writes=[("xres", t8 // 2)], semkey="xf_st")
            if not nospec:
                self.special_proj(l, xf, xfr, t0, 256)

    def xperm(self, g, kc, tb):
        xT = self.xT()
        if g == 0:
            return xT[:, kc, tb * 512:(tb + 1) * 512]
        if g == 1:
            return xT[:, kc, :].rearrange("p (i r) -> p r i", r=4)[:, tb, :]
        return xT[:, kc, :].rearrange("p (i r) -> p r i", r=16)[:, 4 * tb:4 * tb + 4, :]

    def xperm_chunk(self, g, kc, c):
        xT = self.xT()
        if g == 0:
            return xT[:, kc, c * 128:(c + 1) * 128]
        if g == 1:
            return xT[:, kc, :].rearrange("p (i r) -> p r i", r=4)[:, c // 4, (c % 4) * 128:(c % 4 + 1) * 128]
        return xT[:, kc, :].rearrange("p (i r) -> p r i", r=16)[:, c, :]

    def wload(self, src_ap, K, ncols, tag):
        si = self.rot("wb", 2)
        wv = self.wb[si][:, 0:K * ncols].rearrange("p (k n) -> p k n", k=K)
        self.P.dma("pool", wv, src_ap.rearrange("(k p) n -> p k n", p=128), writes=[("wb", si)], semkey="wb%d" % si)
        return wv, ("wb", si)

    def phaseB(self, l):
        P, d = self.P, self.d
        w_in = d["w_in"]
        xall = [("xT", i) for i in range(8)]
        fm_groups = []
        for i in range(3):
            fm_groups.append((C_AQ + 256 * i, 256, [FM_AQ + 2 * i, FM_AQ + 2 * i + 1], i))
        for i in range(3):
            fm_groups.append((C_AK + 256 * i, 256, [FM_AK + 2 * i, FM_AK + 2 * i + 1], i))
        for i in range(2):
            fm_groups.append((C_BQ + 256 * i, 256, [FM_BQ + 2 * i, FM_BQ + 2 * i + 1], 0))
        fm_groups.append((C_BKC, 256, [FM_BKC, FM_BVC], 0))
        fm_groups.append((C_BKS, 128, [FM_BKS], 0))
        fm_groups.append((C_BKW, 128, [FM_BKW], 0))
        for i in range(3):
            fm_groups.append((C_CQ + 256 * i, 256, [FM_CQ + 2 * i, FM_CQ + 2 * i + 1], 0))
        for i in range(3):
            fm_groups.append((C_CK + 256 * i, 256, [FM_CK + 2 * i, FM_CK + 2 * i + 1], 0))
        for (c0, ncol, blks, g) in fm_groups:
            wv, wkey = self.wload(w_in[l, :, c0:c0 + ncol], 16, ncol, "fm")
            for j, blk in enumerate(blks):
                for tb in range(4):
                    b = self.rot("ps8", 8)
                    for kc in range(16):
                        self.mm(self.ps[b][:, 0:512], ("ps", b), wv[:, kc, j * 128:(j + 1) * 128], self.xperm(g, kc, tb),
                                kc == 0, kc == 15, [wkey] + xall)
                    si = self.rot("sbf", 3)
                    sbf = self.sbf[si]
                    P.act(lambda e, sbf=sbf, b=b: e.copy(out=sbf[:, :], in_=self.ps[b][:, :]),
                          reads=[("ps", b)], writes=[("sbf", si)])
                    P.dma("sp", self.s["fm"][blk, :, tb * 512:(tb + 1) * 512], sbf[:, :], reads=[("sbf", si)],
                          writes=[("fm", blk, tb)], semkey="sbf%d" % si)
        tm_groups = [(C_AV, [0, 1], 0), (C_AV + 256, [2, 3], 1), (C_AV + 512, [4, 5], 2), (None, [6, 7], 0),
                     (C_CV, [8, 9], 0), (C_CV + 256, [10, 11], 0), (C_CV + 512, [12, 13], 0)]
        for (c0, hds, g) in tm_groups:
            if c0 is not None:
                wv, wkey = self.wload(w_in[l, :, c0:c0 + 256], 16, 256, "tm")
            else:
                si = self.rot("wb", 2)
                wv = self.wb[si][:, 0:16 * 256].rearrange("p (k n) -> p k n", k=16)
                wkey = ("wb", si)
                P.dma("pool", wv[:, :, 0:128], w_in[l, :, C_BVS:C_BVS + 128].rearrange("(k p) n -> p k n", p=128),
                      writes=[wkey], semkey="wb%d" % si)
                P.dma("pool", wv[:, :, 128:256], w_in[l, :, C_BVW:C_BVW + 128].rearrange("(k p) n -> p k n", p=128),
                      writes=[wkey], semkey="wb%d" % si, join=True)
            for c in range(16):
                b = self.rot("ps8", 8)
                for kc in range(16):
                    self.mm(self.ps[b][:, 0:256], ("ps", b), self.xperm_chunk(g, kc, c), wv[:, kc, :],
                            kc == 0, kc == 15, [wkey] + xall)
                si = self.rot("sbf", 3)
                sbf = self.sbf[si]
                P.act(lambda e, sbf=sbf, b=b: e.copy(out=sbf[:, 0:256], in_=self.ps[b][:, 0:256]),
                      reads=[("ps", b)], writes=[("sbf", si)])
                P.dma("sp", self.s["vt"][hds[0]:hds[0] + 2, c * 128:(c + 1) * 128, :].rearrange("h p e -> p h e"),
                      sbf[:, 0:256].rearrange("p (h e) -> p h e", h=2), reads=[("sbf", si)],
                      writes=[("vt", hds[0]), ("vt", hds[1])], semkey="sbf%d" % si)
        self.load_cols(0, d["b_gate"][l], 48)
        for gi in range(24):
            wv, wkey = self.wload(d["w_gate"][l, :, gi * 256:(gi + 1) * 256], 16, 256, "gate")
            for j in range(2):
                blk = 2 * gi + j
                for tb in range(4):
                    b = self.rot("ps8", 8)
                    for kc in range(16):
                        self.mm(self.ps[b][:, 0:512], ("ps", b), wv[:, kc, j * 128:(j + 1) * 128], self.xperm(0, kc, tb),
                                kc == 0, kc == 15, [wkey] + xall)
                    si = self.rot("sbf", 3)
                    sbf = self.sbf[si]
                    P.act(lambda e, sbf=sbf, b=b, blk=blk: e.activation(out=sbf[:, :], in_=self.ps[b][:, :], func=AF.Sigmoid,
                                                                      bias=self.cols[:, blk:blk + 1]),
                          reads=[("ps", b), "cols"], writes=[("sbf", si)])
                    P.dma("sp", self.s["gt"][blk, :, tb * 512:(tb + 1) * 512], sbf[:, :], reads=[("sbf", si)],
                          writes=[("gt", blk, tb)], semkey="sbf%d" % si)

    def Xv(self, lo, n, parts=128):
        return self.X[0:parts, lo:lo + n]

    def phaseS(self, l):
        P, d = self.P, self.d
        xall = [("xT", i) for i in range(8)]
        gsig = self.Xv(12288, S, 12)
        cT = self.Xv(14336, S, 6)
        l1 = self.Xv(8192, S, 6)
        sm = self.small
        spec_r = [("spec", j, t0) for j in range(2) for t0 in range(0, S, 256)] + \
                 [("spec", j, t0) for j in range(2) for t0 in range(0, S, 512)]
        P.dma("sp", gsig, self.s["spec"][0, 0:12, :], reads=spec_r, writes=["gsig"] + xall, semkey="specld0")
        P.dma("sp", l1, self.s["spec"][1, 0:6, :], reads=spec_r, writes=["l1"] + xall, semkey="specld1")
        P.dma("sp", sm[0:12, 0:1], d["b_nsa_gate"][l].rearrange("(p o) -> p o", o=1), writes=["sm0"], semkey="sm_a")
        P.dma("sp", sm[0:6, 1:2], d["b_f"][l].rearrange("(p o) -> p o", o=1), writes=["sm1"], semkey="sm_b")
        P.act(lambda e: e.activation(out=gsig, in_=gsig, func=AF.Sigmoid, bias=sm[0:12, 0:1]),
              reads=["gsig", "sm0"], writes=["gsig"])
        P.dve(lambda e: e.tensor_scalar(out=sm[0:6, 2:3], in0=sm[0:6, 1:2], scalar1=-1.0, scalar2=None, op0=ALU.mult),
              reads=["sm1"], writes=["sm2"])
        P.act(lambda e: e.activation(out=l1, in_=l1, func=AF.Exp, bias=sm[0:6, 2:3], scale=-1.0),
              reads=["l1", "sm2"], writes=["l1"])
        P.act(lambda e: e.activation(out=l1, in_=l1, func=AF.Ln, bias=1.0), reads=["l1"], writes=["l1"])
        P.dve(lambda e: e.tensor_tensor_scan(out=cT, data0=self.ones_f[0:6, 0:1].to_broadcast([6, S]), data1=l1,
                                             initial=0.0, op0=ALU.mult, op1=ALU.subtract),
              reads=["l1", "ones_f"], writes=["cT"] + xall)
        b = self.rot("ps8", 8)
        for c in range(16):
            self.tr(self.ps[b][:, c * 6:(c + 1) * 6], ("ps", b), cT[0:6, c * 128:(c + 1) * 128], 6, ["cT"])
        P.act(lambda e: e.mul(out=self.negck[:].rearrange("p c h -> p (c h)"), in_=self.ps[b][:, 0:96], mul=-1.0),
              reads=[("ps", b)], writes=["negck"])

    def abuf(self):
        q = [self.Zb(0, 2048), self.Zb(1024, 2048)]
        k = [self.Zb(2048, 2048), self.Zb(3072, 2048)]
        v = [self.Zb(4096, 2048).rearrange("p (c e) -> p c e", c=16), self.Zb(5120, 2048).rearrange("p (c e) -> p c e", c=16)]
        tab = [self.Zf(6144, 2816), self.Zf(8960, 2816)]
        return q, k, v, tab

    def ld_fm(self, kind, blk):
        q, k, v, tab = self.abuf()
        bufs = q if kind == "q" else k
        si = self.rot("ab_" + kind, 2)
        self.P.dma("sp", bufs[si], self.s["fm"][blk], reads=[("fm", blk, tb) for tb in range(4)],
                   writes=[(kind, si)], semkey="%s%d" % (kind, si))
        return bufs[si], (kind, si)

    def ld_v(self, hd):
        q, k, v, tab = self.abuf()
        si = self.rot("ab_v", 2)
        self.P.dma("sp", v[si], self.s["vt"][hd].rearrange("(c p) e -> p c e", p=128), reads=[("vt", hd)],
                   writes=[("v", si)], semkey="v%d" % si)
        return v[si], ("v", si)

    def ld_tab(self, name, h, L, s_, C, M):
        q, k, v, tab = self.abuf()
        si = self.rot("ab_t", 2)
        self.P.dma("sp", tab[si][:, 0:M], self.tab_src(name, h, L, s_, C, M), reads=["ftab"],
                   writes=[("tab", si)], semkey="tab%d" % si)
        return tab[si], ("tab", si)

    def phaseM(self, l):
        P, d = self.P, self.d
        sm = self.small
        hid = [self.sbf[0], self.sbf[1]]
        for c in range(2):
            src, skey = self.ld_fm("q", FM_BKC + c)
            si = self.rot("wb", 2)
            w1v = self.wb[si][:, 0:4096].rearrange("p (l e) -> p l e", l=32)
            P.dma("pool", w1v, d["cmp_w1"][l, c].rearrange("(l d) e -> d l e", d=128), writes=[("wb", si)],
                  semkey="wb%d" % si)
            P.dma("sp", self.sf[1][0:32, 0:128], d["cmp_pe"][l, c], writes=[("sf", 1)], semkey="sfld")
            b = self.rot("ps8", 8)
            self.tr(self.ps[b][:, 0:32], ("ps", b), self.sf[1][0:32, 0:128], 32, [("sf", 1)])
            peT = self.sbf[2][:, 0:32]
            P.act(lambda e, b=b: e.copy(out=peT, in_=self.ps[b][:, 0:32]), reads=[("ps", b)], writes=[("sbf", 2)])
            P.dma("sp", sm[:, 4 + c:5 + c], d["cmp_b1"][l, c].rearrange("(p o) -> p o", o=1), writes=[("smb", c)],
                  semkey="sm_c%d" % c)
            bh = self.rot("ps8", 8)
            for li in range(32):
                self.mm(self.ps[bh][:, 0:N_CMP], ("ps", bh), w1v[:, li, :], src[:, li:li + 2017:16], li == 0, li == 31,
                        [("wb", si), skey])
            bp = self.rot("ps8", 8)
            for li in range(32):
                self.mm(self.ps[bp][:, 0:1], ("ps", bp), w1v[:, li, :], peT[:, li:li + 1], li == 0, li == 31,
                        [("wb", si), ("sbf", 2)])
            P.dve(lambda e, bp=bp, c=c: e.tensor_tensor(out=sm[:, 8 + c:9 + c], in0=self.ps[bp][:, 0:1], in1=sm[:, 4 + c:5 + c],
                                                       op=ALU.add), reads=[("ps", bp), ("smb", c)], writes=[("smh", c)])
            P.act(lambda e, bh=bh, c=c: e.activation(out=hid[c][:, 0:N_CMP], in_=self.ps[bh][:, 0:N_CMP], func=AF.Gelu,
                                                    bias=sm[:, 8 + c:9 + c]),
                  reads=[("ps", bh), ("smh", c)], writes=[("sbf", c)])
        si = self.rot("wb", 2)
        w2 = self.wb[si][:, 0:256].rearrange("p (c e) -> p c e", c=2)
        P.dma("pool", w2, d["cmp_w2"][l].rearrange("c e x -> e c x"), writes=[("wb", si)], semkey="wb%d" % si)
        P.dma("sp", sm[:, 6:7], d["cmp_b2"][l, 0].rearrange("(p o) -> p o", o=1), writes=["smb2"], semkey="sm_d")
        b2row = self.sbf[2][0:1, 128:256]
        P.dma("pool", b2row, d["cmp_b2"][l, 1].rearrange("(o e) -> o e", o=1), writes=[("sbf", 2)], semkey="b2row")
        b = self.rot("ps8", 8)
        self.mm(self.ps[b][:, 0:N_CMP], ("ps", b), w2[:, 0, :], hid[0][:, 0:N_CMP], True, True, [("wb", si), ("sbf", 0)])
        P.act(lambda e, b=b: e.activation(out=self.kcT[:, 0:N_CMP], in_=self.ps[b][:, 0:N_CMP], func=AF.Identity,
                                          bias=sm[:, 6:7]), reads=[("ps", b), "smb2"], writes=["kcT"])
        b = self.rot("ps8", 8)
        self.mm(self.ps[b][0:N_CMP, 0:128], ("ps", b), hid[1][:, 0:N_CMP], w2[:, 1, :], True, False, [("wb", si), ("sbf", 1)])
        self.mm(self.ps[b][0:N_CMP, 0:128], ("ps", b), self.ones_bf[0:1, 0:N_CMP], b2row, False, True,
                [("sbf", 2), "ones_bf"])
        P.act(lambda e, b=b: e.copy(out=self.vc[0:N_CMP, :], in_=self.ps[b][0:N_CMP, 0:128]), reads=[("ps", b)],
              writes=["vc"])

    def attn_tile(self, lhsT_k, rhs_q, nk, nq, bias_ap, v_lhsT, ao, as_, first, last, reads_kq, reads_b, reads_v,
                  keybias=None, kb_reads=(), extra=None, sel=None, post=None):
        P = self.P
        sbk = self.sbank()
        ps_s = self.ps[sbk][0:nk, 0:nq]
        self.mm(ps_s, ("ps", sbk), lhsT_k, rhs_q, True, sel is None, reads_kq)
        if sel is not None:
            e_ap, selb_ap, sreads = sel
            self.mm(ps_s, ("ps", sbk), e_ap, selb_ap, False, True, sreads)
        ti = self.rot("tf", 3)
        tf = self.atf()[ti][0:nk, 0:nq]
        P.dve(lambda e: e.scalar_tensor_tensor(out=tf, in0=ps_s, scalar=SCALE, in1=bias_ap, op0=ALU.mult, op1=ALU.add),
              reads=[("ps", sbk)] + list(reads_b), writes=[("tf", ti)])
        if extra is not None:
            m_ap, mreads = extra
            P.dve(lambda e: e.tensor_tensor(out=tf, in0=tf, in1=m_ap, op=ALU.add), reads=[("tf", ti)] + list(mreads),
                  writes=[("tf", ti)])
        pi = self.rot("pT", 7)
        pT = self.apT()[pi][0:nk, 0:nq]
        if keybias is None:
            P.act(lambda e: e.activation(out=pT, in_=tf, func=AF.Exp), reads=[("tf", ti)], writes=[("pT", pi)])
        else:
            P.act(lambda e: e.activation(out=pT, in_=tf, func=AF.Exp, bias=keybias), reads=[("tf", ti)] + list(kb_reads),
                  writes=[("pT", pi)])
        ones = self.ones_bf[0:nk, :]
        rv = list(reads_v)

        def pv():
            self.mm(self.ps[ao][:, 0:nq], ("ps", ao), v_lhsT, pT, first, last, [("pT", pi)] + rv)
            self.mm(self.ps[as_][:, 0:nq], ("ps", as_), ones, pT, first, last, [("pT", pi), "ones_bf"])
            if post is not None:
                post(pT, ("pT", pi))
        self.defer(pv)
        return pT, ("pT", pi)

    def sbank(self):
        return (0, 1, 2, 7)[self.rot("S", 4)]

    def atf(self):
        return [self.tf[0], self.tf[1], self.u[3]]

    def apT(self):
        return [self.pT[0], self.pT[1], self.pT[2], self.u[4][:, 0:256].bitcast(BF16), self.u[4][:, 256:512].bitcast(BF16),
                self.u[5][:, 0:256].bitcast(BF16), self.u[5][:, 256:512].bitcast(BF16)]

    def defer(self, fn, la=6):
        self.pend.append(fn)
        while len(self.pend) > la:
            self.pend.pop(0)()

    def after(self, fn):
        self.pend.append(fn)

    def flush(self):
        while self.pend:
            self.pend.pop(0)()

    def acc_banks(self):
        i = self.rot("acc", 2)
        return 3 + i, 5 + i

    def store_ot(self, src_f32_ap, skeys, oi, t0, n, eng="act"):
        P = self.P
        si = self.rot("sbf", 3)
        sbf = self.sbf[si]
        if eng == "act":
            P.act(lambda e: e.copy(out=sbf[:, 0:n], in_=src_f32_ap), reads=list(skeys), writes=[("sbf", si)])
        else:
            P.pool(lambda e: e.tensor_copy(out=sbf[:, 0:n], in_=src_f32_ap), reads=list(skeys), writes=[("sbf", si)])
        P.dma("sp", self.s["ot"][oi, :, t0:t0 + n], sbf[:, 0:n], reads=[("sbf", si)], writes=[("ot", oi, t0)],
              semkey="sbf%d" % si)

    def mixerA(self, l):
        P = self.P
        Uacc = self.Xv(8192, S)
        sacc = self.Xv(10240, S)
        import os
        glist = [int(c) for c in os.environ.get("AGROUPS", "012")]
        for hh in range(2):
            for g in glist:
                h = 2 * g + hh
                qT, qk = self.ld_fm("q", FM_AQ + h)
                kT, kk = self.ld_fm("k", FM_AK + h)
                v, vk = self.ld_v(VT_A + h)
                tab, tk = self.ld_tab("fa", h, L_A, 1, 127, M_A)
                if g < 2:
                    for qb in range(4):
                        ao, as_ = self.acc_banks()
                        if g == 0:
                            chunks = [(qb * 4 - 1 + c) for c in range(5) if qb * 4 - 1 + c >= 0]
                            offs = {kc: qb * 512 - kc * 128 for kc in chunks}
                        else:
                            chunks = [4 * qb + c for c in range(4)]
                            offs = {kc: -(kc - 4 * qb) * 128 for kc in chunks}
                        for i, kc in enumerate(chunks):
                            m0 = offs[kc] + 384
                            self.attn_tile(kT[:, kc * 128:(kc + 1) * 128], qT[:, qb * 512:(qb + 1) * 512], 128, 512,
                                           tab[:, m0:m0 + 512], v[:, kc, :], ao, as_, i == 0, i == len(chunks) - 1,
                                           [qk, kk], [tk], [vk])
                        def accum(g=g, qb=qb, ao=ao, as_=as_):
                            if g == glist[0] and g == 0:
                                uo = Uacc[:, qb * 512:(qb + 1) * 512]
                                so = sacc[:, qb * 512:(qb + 1) * 512]
                                P.act(lambda e: e.copy(out=uo, in_=self.ps[ao][:, :]), reads=[("ps", ao)], writes=["Uacc"])
                                P.act(lambda e: e.copy(out=so, in_=self.ps[as_][:, :]), reads=[("ps", as_)], writes=["sacc"])
                            else:
                                uo = Uacc.rearrange("p (i r) -> p r i", r=4)[:, qb, :]
                                so = sacc.rearrange("p (i r) -> p r i", r=4)[:, qb, :]
                                P.dve(lambda e: e.tensor_tensor(out=uo, in0=uo, in1=self.ps[ao][:, :], op=ALU.add),
                                      reads=[("ps", ao), "Uacc"], writes=["Uacc"])
                                P.dve(lambda e: e.tensor_tensor(out=so, in0=so, in1=self.ps[as_][:, :], op=ALU.add),
                                      reads=[("ps", as_), "sacc"], writes=["sacc"])
                        self.after(accum)
                else:
                    for b4 in range(4):
                        ao, as_ = self.acc_banks()
                        sbk = self.sbank()
                        for i in range(4):
                            r = 4 * b4 + i
                            self.mm(self.ps[sbk][:, i * 128:(i + 1) * 128], ("ps", sbk), kT[:, r * 128:(r + 1) * 128],
                                    qT[:, r * 128:(r + 1) * 128], True, True, [qk, kk])
                        ti = self.rot("tf", 3)
                        tf = self.atf()[ti]
                        for i in range(4):
                            P.dve(lambda e, i=i, tf=tf, sbk=sbk, tab=tab: e.scalar_tensor_tensor(
                                out=tf[:, i * 128:(i + 1) * 128], in0=self.ps[sbk][:, i * 128:(i + 1) * 128], scalar=SCALE,
                                in1=tab[:, 384:512], op0=ALU.mult, op1=ALU.add),
                                reads=[("ps", sbk), tk], writes=[("tf", ti)])
                        pi = self.rot("pT", 7)
                        pT = self.apT()[pi]
                        P.act(lambda e, pT=pT, tf=tf: e.activation(out=pT[:, :], in_=tf[:, :], func=AF.Exp),
                              reads=[("tf", ti)], writes=[("pT", pi)])
                        def pv2(b4=b4, ao=ao, as_=as_, pT=pT, pi=pi, v=v, vk=vk):
                            for i in range(4):
                                r = 4 * b4 + i
                                self.mm(self.ps[ao][:, i * 128:(i + 1) * 128], ("ps", ao), v[:, r, :], pT[:, i * 128:(i + 1) * 128],
                                        True, True, [("pT", pi), vk])
                                self.mm(self.ps[as_][:, i * 128:(i + 1) * 128], ("ps", as_), self.ones_bf[:, :],
                                        pT[:, i * 128:(i + 1) * 128], True, True, [("pT", pi), "ones_bf"])
                            uo = Uacc.rearrange("p (i r) -> p r i", r=16)[:, 4 * b4:4 * b4 + 4, :]
                            so = sacc.rearrange("p (i r) -> p r i", r=16)[:, 4 * b4:4 * b4 + 4, :]
                            pso = self.ps[ao][:, :].rearrange("p (r i) -> p r i", r=4)
                            pss = self.ps[as_][:, :].rearrange("p (r i) -> p r i", r=4)
                            P.dve(lambda e: e.tensor_tensor(out=uo, in0=uo, in1=pso, op=ALU.add),
                                  reads=[("ps", ao), "Uacc"], writes=["Uacc"])
                            P.dve(lambda e: e.tensor_tensor(out=so, in0=so, in1=pss, op=ALU.add),
                                  reads=[("ps", as_), "sacc"], writes=["sacc"])
                        self.defer(pv2)
            self.flush()
            for qb in range(4):
                sl = slice(qb * 512, (qb + 1) * 512)
                P.dve(lambda e, sl=sl: e.reciprocal(out=sacc[:, sl], in_=sacc[:, sl]), reads=["sacc"], writes=["sacc"])
                P.dve(lambda e, sl=sl: e.tensor_tensor(out=Uacc[:, sl], in0=Uacc[:, sl], in1=sacc[:, sl], op=ALU.mult),
                      reads=["sacc", "Uacc"], writes=["Uacc"])
                self.store_ot(Uacc[:, sl], ["Uacc"], hh, qb * 512, 512)

    def gate_bcast(self, row, qb):
        gsig = self.Xv(12288, S, 12)
        gbk = self.sbank()
        self.mm(self.ps[gbk][:, 0:512], ("ps", gbk), self.sel12[:, row, :], gsig[:, qb * 512:(qb + 1) * 512], True, True,
                ["sel12", "gsig"])
        return gbk

    def finish_branch(self, ao, as_, br, h, qb, first):
        P = self.P
        oB = self.Xv(0, 4 * S).rearrange("p (h t) -> p h t", h=4)
        ri = self.rot("rs", 2)
        rs = self.rs[ri]
        P.act(lambda e: e.activation(out=rs[:, :], in_=self.ps[as_][:, :], func=AF.Ln, bias=self.tiny[:, 0:1]),
              reads=[("ps", as_), "tiny"], writes=[("rs", ri)])
        P.act(lambda e: e.activation(out=rs[:, :], in_=rs[:, :], func=AF.Exp, scale=-1.0), reads=[("rs", ri)],
              writes=[("rs", ri)])
        gbk = self.gate_bcast(br * 4 + h, qb)
        P.dve(lambda e: e.tensor_tensor(out=rs[:, :], in0=rs[:, :], in1=self.ps[gbk][:, :], op=ALU.mult),
              reads=[("rs", ri), ("ps", gbk)], writes=[("rs", ri)])
        dst = oB[:, h, qb * 512:(qb + 1) * 512]
        if first:
            P.dve(lambda e: e.tensor_tensor(out=dst, in0=self.ps[ao][:, :], in1=rs[:, :], op=ALU.mult),
                  reads=[("ps", ao), ("rs", ri)], writes=[("oB", h, qb)])
        else:
            P.dve(lambda e: e.tensor_tensor(out=rs[:, :], in0=self.ps[ao][:, :], in1=rs[:, :], op=ALU.mult),
                  reads=[("ps", ao), ("rs", ri)], writes=[("rs", ri)])
            P.pool(lambda e: e.tensor_tensor(out=dst, in0=dst, in1=rs[:, :], op=ALU.add),
                   reads=[("rs", ri), ("oB", h, qb)], writes=[("oB", h, qb)])

    def mixerB(self, l):
        P = self.P
        imp = self.imp
        qs = []
        for h in range(4):
            qT, qk = self.ld_fm("q", FM_BQ + h)
            tab, tk = self.ld_tab("fc", h, L_C, 16, 2016, M_C)
            for qb in range(4):
                ao, as_ = self.acc_banks()
                post = None
                if qb >= 2:
                    def post(pT, pk, h=h, qb=qb):
                        bi = self.sbank()
                        for sub in range(4):
                            self.mm(self.ps[bi][:, sub * 33:(sub + 1) * 33], ("ps", bi), pT[:, sub * 128:(sub + 1) * 128],
                                    self.ovx[0:N_CMP, :], True, True, [pk, "ovx"])
                        sm = self.small
                        for sub in range(4):
                            qc = qb * 4 + sub
                            P.dve(lambda e, sub=sub: e.tensor_scalar(out=sm[:, 16:17], in0=self.ps[bi][:, sub * 33 + 32:sub * 33 + 33],
                                                                     scalar1=1e-30, scalar2=None, op0=ALU.add),
                                  reads=[("ps", bi)], writes=["sm16"])
                            P.dve(lambda e: e.reciprocal(out=sm[:, 16:17], in_=sm[:, 16:17]), reads=["sm16"], writes=["sm16"])
                            if h == 0:
                                P.dve(lambda e, sub=sub, qc=qc: e.tensor_scalar(
                                    out=imp[:, qc, :], in0=self.ps[bi][:, sub * 33:sub * 33 + 32], scalar1=sm[:, 16:17],
                                    scalar2=None, op0=ALU.mult), reads=[("ps", bi), "sm16"], writes=[("imp", qc)])
                            else:
                                P.dve(lambda e, sub=sub, qc=qc: e.scalar_tensor_tensor(
                                    out=imp[:, qc, :], in0=self.ps[bi][:, sub * 33:sub * 33 + 32], scalar=sm[:, 16:17],
                                    in1=imp[:, qc, :], op0=ALU.mult, op1=ALU.add),
                                    reads=[("ps", bi), "sm16", ("imp", qc)], writes=[("imp", qc)])
                self.attn_tile(self.kcT[:, 0:N_CMP], qT[:, qb * 512:(qb + 1) * 512], N_CMP, 512,
                               tab[0:N_CMP, qb * 512:(qb + 1) * 512], self.vc[0:N_CMP, :], ao, as_, True, True,
                               [qk, "kcT"], [tk], ["vc"], post=post)
                self.after(lambda ao=ao, as_=as_, h=h, qb=qb: self.finish_branch(ao, as_, 0, h, qb, True))
        self.flush()
        sm = self.small
        for qc in range(8, 16):
            sc = sm[:, 20:52]
            P.dve(lambda e, qc=qc: e.tensor_tensor(out=sc, in0=imp[:, qc, :], in1=self.selmul[:, qc, :], op=ALU.mult),
                  reads=[("imp", qc), "selmul"], writes=["sc"])
            P.dve(lambda e, qc=qc: e.tensor_tensor(out=sc, in0=sc, in1=self.seladd[:, qc, :], op=ALU.add),
                  reads=["sc", "seladd"], writes=["sc"])
            wk = self.u[0][:, 0:32]
            m8 = self.u[0][:, 32:40]
            P.dve(lambda e: e.max(out=m8, in_=sc), reads=["sc"], writes=["m8"])
            P.dve(lambda e: e.match_replace(out=wk, in_to_replace=m8, in_values=sc, imm_value=-3.0e38),
                  reads=["sc", "m8"], writes=["wk"])
            P.dve(lambda e: e.max(out=m8, in_=wk), reads=["wk", "m8"], writes=["m8"])
            selb = self.u[0][:, 64:96]
            P.dve(lambda e: e.tensor_scalar(out=selb, in0=sc, scalar1=m8[:, 7:8], scalar2=None, op0=ALU.is_ge),
                  reads=["sc", "m8"], writes=["selb"])
            P.dve(lambda e: e.tensor_scalar(out=selb, in0=selb, scalar1=-1.0, scalar2=-NEG, op0=ALU.add, op1=ALU.mult),
                  reads=["selb"], writes=["selb"])
            b = self.sbank()
            self.tr(self.ps[b][0:32, 0:128], ("ps", b), selb, 128, ["selb"])
            P.act(lambda e, b=b, qc=qc: e.copy(out=self.selbT[:, qc * 128:(qc + 1) * 128], in_=self.ps[b][0:32, 0:128]),
                  reads=[("ps", b)], writes=["selbT"])
        for br, (fmk, vth, tname, L, M) in ((1, (FM_BKS, VT_BS, "fs", L_S, M_S)), (2, (FM_BKW, VT_BW, "fw", L_W, M_W))):
            kT, kk = self.ld_fm("k", fmk)
            v, vk = self.ld_v(vth)
            for h in range(4):
                qT, qk = self.ld_fm("q", FM_BQ + h)
                tab, tk = self.ld_tab(tname, h, L, 1, 127, M)
                for qb in range(4):
                    ao, as_ = self.acc_banks()
                    if br == 1:
                        chunks = list(range(0, 4 * (qb + 1)))
                    else:
                        chunks = list(range(max(0, 4 * qb - 4), 4 * qb + 4))
                    for i, kc in enumerate(chunks):
                        m0 = qb * 512 - kc * 128 + 384
                        sel = None
                        if br == 1 and qb >= 2:
                            sel = (self.eall[:, kc * 128:(kc + 1) * 128], self.selbT[:, qb * 512:(qb + 1) * 512],
                                   ["eall", "selbT"])
                        self.attn_tile(kT[:, kc * 128:(kc + 1) * 128], qT[:, qb * 512:(qb + 1) * 512], 128, 512,
                                       tab[:, m0:m0 + 512], v[:, kc, :], ao, as_, i == 0, i == len(chunks) - 1,
                                       [qk, kk], [tk], [vk], sel=sel)
                    self.after(lambda ao=ao, as_=as_, br=br, h=h, qb=qb: self.finish_branch(ao, as_, br, h, qb, False))
        self.flush()
        oB = self.Xv(0, 4 * S).rearrange("p (h t) -> p h t", h=4)
        for h in range(4):
            for qb in range(4):
                self.store_ot(oB[:, h, qb * 512:(qb + 1) * 512], [("oB", h, qb)], 2 + h, qb * 512, 512, eng="pool")

    def mixerC(self, l):
        P = self.P
        cT = self.Xv(14336, S, 6)
        cqbs = [self.Xv(8192, 512), self.Xv(8704, 512)]
        for h in range(6):
            qT, qk = self.ld_fm("q", FM_CQ + h)
            kT, kk = self.ld_fm("k", FM_CK + h)
            v, vk = self.ld_v(VT_C + h)
            for qb in range(4):
                ci = self.rot("cqb", 2)
                cqb = cqbs[ci]
                cbk = self.sbank()
                self.mm(self.ps[cbk][:, 0:512], ("ps", cbk), self.sel12[0:6, h, :], cT[:, qb * 512:(qb + 1) * 512], True, True,
                        ["sel12", "cT"])
                P.act(lambda e, cbk=cbk, cqb=cqb: e.copy(out=cqb, in_=self.ps[cbk][:, :]), reads=[("ps", cbk)], writes=[("cqb", ci)])
                ao, as_ = self.acc_banks()
                nch = 4 * (qb + 1)
                for kc in range(nch):
                    extra = None
                    if kc >= 4 * qb:
                        m0 = qb * 512 - kc * 128 + 384
                        extra = (self.cmask[:, m0:m0 + 512], ["cmask"])
                    self.attn_tile(kT[:, kc * 128:(kc + 1) * 128], qT[:, qb * 512:(qb + 1) * 512], 128, 512, cqb,
                                   v[:, kc, :], ao, as_, kc == 0, kc == nch - 1, [qk, kk], [("cqb", ci)], [vk],
                                   keybias=self.negck[:, kc, h:h + 1], kb_reads=["negck"], extra=extra)
                def fin(h=h, qb=qb, ao=ao, as_=as_):
                    ri = self.rot("rs", 2)
                    rs = self.rs[ri]
                    P.act(lambda e: e.activation(out=rs[:, :], in_=self.ps[as_][:, :], func=AF.Ln), reads=[("ps", as_)],
                          writes=[("rs", ri)])
                    P.act(lambda e: e.activation(out=rs[:, :], in_=rs[:, :], func=AF.Exp, scale=-1.0), reads=[("rs", ri)],
                          writes=[("rs", ri)])
                    P.dve(lambda e: e.tensor_tensor(out=rs[:, :], in0=self.ps[ao][:, :], in1=rs[:, :], op=ALU.mult),
                          reads=[("ps", ao), ("rs", ri)], writes=[("rs", ri)])
                    self.store_ot(rs[:, :], [("rs", ri)], 6 + h, qb * 512, 512)
                self.after(fin)
        self.flush()

    def ln_stats(self, rbuf, rkeys, dc, n):
        P = self.P
        bs, bq = 5, 6
        ui = self.rot("u", 6)
        u = self.u[ui]
        P.act(lambda e: e.activation(out=u[:, 0:n], in_=rbuf[:, dc, :], func=AF.Square), reads=[rkeys[dc]], writes=[("u", ui)])

        def pe_part():
            self.mm(self.ps[bs][:, 0:n], ("ps", bs), self.ones_f[:, :], rbuf[:, dc, :], dc == 0, dc == 15, ["ones_f", rkeys[dc]])
            self.mm(self.ps[bq][:, 0:n], ("ps", bq), self.ones_f[:, :], u[:, 0:n], dc == 0, dc == 15, ["ones_f", ("u", ui)])
        return pe_part

    def ln_scale(self, n):
        P = self.P
        bs, bq = 5, 6
        mean = self.tf[0][:, 0:n]
        rstd = self.tf[1][:, 0:n]
        P.dve(lambda e: e.tensor_scalar(out=mean, in0=self.ps[bs][:, 0:n], scalar1=1.0 / D, scalar2=None, op0=ALU.mult),
              reads=[("ps", bs)], writes=[("tf", 0)])
        P.dve(lambda e: e.tensor_tensor(out=rstd, in0=mean, in1=mean, op=ALU.mult), reads=[("tf", 0)], writes=[("tf", 1)])
        P.dve(lambda e: e.scalar_tensor_tensor(out=rstd, in0=self.ps[bq][:, 0:n], scalar=1.0 / D, in1=rstd, op0=ALU.mult,
                                               op1=ALU.subtract), reads=[("ps", bq), ("tf", 1)], writes=[("tf", 1)])
        P.dve(lambda e: e.tensor_scalar(out=rstd, in0=rstd, scalar1=LN_EPS, scalar2=None, op0=ALU.add),
              reads=[("tf", 1)], writes=[("tf", 1)])
        P.act(lambda e: e.activation(out=rstd, in_=rstd, func=AF.Ln), reads=[("tf", 1)], writes=[("tf", 1)])
        P.act(lambda e: e.activation(out=rstd, in_=rstd, func=AF.Exp, scale=-0.5), reads=[("tf", 1)], writes=[("tf", 1)])
        return mean, rstd

    def ln_norm(self, rbuf, rkeys, dc, gcol, bcol, n, mean, rstd, xbt=None):
        P = self.P
        ui = self.rot("u", 6)
        u = self.u[ui]
        P.dve(lambda e: e.tensor_tensor(out=u[:, 0:n], in0=rbuf[:, dc, :], in1=mean, op=ALU.subtract),
              reads=[rkeys[dc], ("tf", 0)], writes=[("u", ui)])
        P.dve(lambda e: e.tensor_tensor(out=u[:, 0:n], in0=u[:, 0:n], in1=rstd, op=ALU.mult),
              reads=[("u", ui), ("tf", 1)], writes=[("u", ui)])
        P.act(lambda e: e.activation(out=rbuf[:, dc, :], in_=u[:, 0:n], func=AF.Identity,
                                     bias=self.cols[:, bcol + dc:bcol + dc + 1], scale=self.cols[:, gcol + dc:gcol + dc + 1]),
              reads=[("u", ui), "cols"], writes=[rkeys[dc]])
        if xbt is not None:
            P.act(lambda e: e.activation(out=xbt[:, dc, :], in_=u[:, 0:n], func=AF.Identity,
                                         bias=self.cols[:, bcol + dc:bcol + dc + 1], scale=self.cols[:, gcol + dc:gcol + dc + 1]),
                  reads=[("u", ui), "cols"], writes=[("xbt", dc)])

    def ld_rbuf(self, rbuf, rkeys, src, ts, rd):
        for q4 in range(4):
            self.ld_rbuf_chunk(rbuf, rkeys, src, ts, rd, q4)

    def ld_rbuf_chunk(self, rbuf, rkeys, src, ts, rd, q4):
        self.P.dma("sp", rbuf[:, 4 * q4:4 * q4 + 4, :], src[4 * q4:4 * q4 + 4, :, ts].rearrange("k p t -> p k t"),
                   reads=rd, writes=rkeys[4 * q4:4 * q4 + 4], semkey="rbufld%d" % q4)

    def phaseD1(self, l):
        P, d = self.P, self.d
        oT = self.Zb(0, 12 * S).rearrange("p (k t) -> p k t", k=12)
        mixT = self.xT()
        P.dma("sp", oT, self.s["ot"].rearrange("k p t -> p k t"),
              reads=[("ot", oi, t0) for oi in range(12) for t0 in range(0, S, 512)], writes=["oT"], semkey="otld")
        gts = self.s["gt"].rearrange("(m c) p t -> c p m t", m=3)
        for dcp in range(8):
            si = self.rot("wb", 2)
            wv = self.wb[si][:, 0:12 * 256].rearrange("p (k n) -> p k n", k=12)
            cs = slice(dcp * 256, (dcp + 1) * 256)
            P.dma("pool", wv[:, 0:2, :], d["w_pa"][l, :, cs].rearrange("(k p) n -> p k n", p=128), writes=[("wb", si)],
                  semkey="wb%d" % si)
            P.dma("pool", wv[:, 2:6, :], d["w_pb"][l, :, cs].rearrange("(k p) n -> p k n", p=128), writes=[("wb", si)],
                  semkey="wb%d" % si, join=True)
            P.dma("pool", wv[:, 6:12, :], d["w_pc"][l, :, cs].rearrange("(k p) n -> p k n", p=128), writes=[("wb", si)],
                  semkey="wb%d" % si, join=True)
            wkeys = [("wb", si)]
            for j in range(2):
                dc = 2 * dcp + j
                for tb in range(4):
                    gi = self.rot("gbuf", 2)
                    gb = self.gbuf[gi][:, :].rearrange("p (m t) -> p m t", m=3)
                    P.dma("sp", gb, gts[dc, :, :, tb * 512:(tb + 1) * 512],
                          reads=[("gt", m * 16 + dc, tb) for m in range(3)], writes=[("gbuf", gi)], semkey="gbuf%d" % gi)
                    banks = []
                    for m, (k0, k1) in enumerate(((0, 2), (2, 6), (6, 12))):
                        b = self.rot("ps8", 8)
                        banks.append(b)
                        for kc in range(k0, k1):
                            self.mm(self.ps[b][:, :], ("ps", b), wv[:, kc, j * 128:(j + 1) * 128],
                                    oT[:, kc, tb * 512:(tb + 1) * 512], kc == k0, kc == k1 - 1, wkeys + ["oT"])
                    ts = []
                    for m in range(3):
                        ui = self.rot("u", 6)
                        u = self.u[ui]
                        ts.append((u, ui))
                        P.dve(lambda e, u=u, m=m, b=banks[m], gb=gb: e.tensor_tensor(out=u[:, :], in0=self.ps[b][:, :],
                                                                                     in1=gb[:, m, :], op=ALU.mult),
                              reads=[("ps", banks[m]), ("gbuf", gi)], writes=[("u", ui)])
                    (u0, i0), (u1, i1), (u2, i2) = ts
                    P.dve(lambda e, u0=u0, u1=u1: e.tensor_tensor(out=u0[:, :], in0=u0[:, :], in1=u1[:, :], op=ALU.add),
                          reads=[("u", i0), ("u", i1)], writes=[("u", i0)])
                    P.dve(lambda e, u0=u0, u2=u2, dc=dc, tb=tb: e.tensor_tensor(out=mixT[:, dc, tb * 512:(tb + 1) * 512],
                                                                                in0=u0[:, :], in1=u2[:, :], op=ALU.add),
                          reads=[("u", i0), ("u", i2)], writes=[("mixT", tb)])

    def phaseD2(self, l):
        P, d = self.P, self.d
        mixT = self.xT()
        rbuf = self.Zf(0, 8192).rearrange("p (k t) -> p k t", k=16)
        xbt = self.Zb(8192, 8192).rearrange("p (k t) -> p k t", k=16)
        rkeys = [("rbuf", dc) for dc in range(16)]
        xres_r = [("xres", i) for i in range(4)] + [("xres", i, q) for i in range(4) for q in range(4)]
        self.ld_rbuf(rbuf, rkeys, self.s["xres"], slice(0, 512), xres_r)
        for tb in range(4):
            ts = slice(tb * 512, (tb + 1) * 512)
            pend = []
            for dcp in range(8):
                wv, wkey = self.wload(d["w_out"][l, :, dcp * 256:(dcp + 1) * 256], 16, 256, "wout")
                for j in range(2):
                    dc = 2 * dcp + j
                    b = self.rot("ps8", 4)
                    for kc in range(16):
                        self.mm(self.ps[b][:, :], ("ps", b), wv[:, kc, j * 128:(j + 1) * 128], mixT[:, kc, ts],
                                kc == 0, kc == 15, [wkey, ("mixT", tb)])
                    P.dve(lambda e, dc=dc, b=b: e.scalar_tensor_tensor(out=rbuf[:, dc, :], in0=rbuf[:, dc, :], scalar=ALPHA,
                                                                       in1=self.ps[b][:, :], op0=ALU.mult, op1=ALU.add),
                          reads=[("ps", b), ("rbuf", dc)], writes=[("rbuf", dc)])
                    pend.append(self.ln_stats(rbuf, rkeys, dc, 512))
                    if len(pend) > 2:
                        pend.pop(0)()
            while pend:
                pend.pop(0)()
            mean, rstd = self.ln_scale(512)
            nts = slice((tb + 1) * 512, (tb + 2) * 512)
            for q4 in range(4):
                for dc in range(4 * q4, 4 * q4 + 4):
                    self.ln_norm(rbuf, rkeys, dc, 416, 432, 512, mean, rstd, xbt=xbt)
                P.dma("sp", self.s["x1f"][4 * q4:4 * q4 + 4, :, ts].rearrange("k p t -> p k t"), rbuf[:, 4 * q4:4 * q4 + 4, :],
                      reads=rkeys[4 * q4:4 * q4 + 4], writes=[("x1f", tb, q4)], semkey="rbufst%d" % q4)
                P.dma("sp", self.s["x1b"][4 * q4:4 * q4 + 4, :, ts].rearrange("k p t -> p k t"), xbt[:, 4 * q4:4 * q4 + 4, :],
                      reads=[("xbt", dc) for dc in range(4 * q4, 4 * q4 + 4)], writes=[("x1b", tb, q4)], semkey="xbtst%d" % q4)
                if tb < 3 and q4 >= 1:
                    self.ld_rbuf_chunk(rbuf, rkeys, self.s["xres"], nts, xres_r, q4 - 1)
            if tb < 3:
                self.ld_rbuf_chunk(rbuf, rkeys, self.s["xres"], nts, xres_r, 3)

    def phaseE1(self, l):
        P, d = self.P, self.d
        xT = self.xT()
        P.dma("sp", xT, self.s["x1b"].rearrange("k p t -> p k t"), reads=[("x1b", tb, q4) for tb in range(4) for q4 in range(4)], writes=["x1T"],
              semkey="xTld")
        cols = self.cols
        hr = self.hraw
        for fc in range(NFC):
            si = self.rot("wb", 2)
            wv = self.wb[si][:, 0:4096].rearrange("p (k a n) -> p k a n", k=16, a=2)
            P.dma("pool", wv[:, :, 0, :], d["w_up"][l, :, fc * 128:(fc + 1) * 128].rearrange("(k p) n -> p k n", p=128),
                  writes=[("wb", si)], semkey="wb%d" % si)
            P.dma("pool", wv[:, :, 1, :],
                  d["w_up"][l, :, D_FF + fc * 128:D_FF + (fc + 1) * 128].rearrange("(k p) n -> p k n", p=128),
                  writes=[("wb", si)], semkey="wb%d" % si, join=True)
            wkeys = [("wb", si)]
            for tb in range(4):
                ts = slice(tb * 512, (tb + 1) * 512)
                us = []
                for ab in range(2):
                    b = self.rot("ps8", 8)
                    for kc in range(16):
                        self.mm(self.ps[b][:, :], ("ps", b), wv[:, kc, ab, :], xT[:, kc, ts], kc == 0, kc == 15,
                                wkeys + ["x1T"])
                    h = hr[ab]
                    if tb == 0:
                        P.dve(lambda e, h=h: e.memset(h[:, 0:2], 0.0), reads=[("hraw", ab)], writes=[("hraw", ab)])
                    else:
                        P.act(lambda e, h=h: e.copy(out=h[:, 0:2], in_=h[:, 512:514]), reads=[("hraw", ab)],
                              writes=[("hraw", ab)])
                    P.act(lambda e, h=h, b=b: e.copy(out=h[:, 2:514], in_=self.ps[b][:, :]), reads=[("ps", b), ("hraw", ab)],
                          writes=[("hraw", ab)])
                    ui = self.rot("u", 6)
                    u = self.u[ui]
                    us.append((u, ui))
                    f = fc + 44 * ab
                    w0 = cols[:, 64 + f:65 + f]
                    w1 = cols[:, 64 + 88 + f:65 + 88 + f]
                    w2 = cols[:, 64 + 176 + f:65 + 176 + f]
                    cb = cols[:, 328 + f:329 + f]
                    P.act(lambda e, u=u, b=b, w2=w2, cb=cb: e.activation(out=u[:, :], in_=self.ps[b][:, :], func=AF.Identity,
                                                                         bias=cb, scale=w2),
                          reads=[("ps", b), "cols"], writes=[("u", ui)])
                    P.dve(lambda e, u=u, h=h, w1=w1: e.scalar_tensor_tensor(out=u[:, :], in0=h[:, 1:513], scalar=w1, in1=u[:, :],
                                                                           op0=ALU.mult, op1=ALU.add),
                          reads=[("hraw", ab), ("u", ui), "cols"], writes=[("u", ui)])
                    P.dve(lambda e, u=u, h=h, w0=w0: e.scalar_tensor_tensor(out=u[:, :], in0=h[:, 0:512], scalar=w0, in1=u[:, :],
                                                                           op0=ALU.mult, op1=ALU.add),
                          reads=[("hraw", ab), ("u", ui), "cols"], writes=[("u", ui)])
                (ua, ia), (ub, ib) = us
                P.act(lambda e, ua=ua: e.activation(out=ua[:, :], in_=ua[:, :], func=AF.Gelu), reads=[("u", ia)],
                      writes=[("u", ia)])
                si2 = self.rot("sbf", 3)
                sbf = self.sbf[si2]
                P.dve(lambda e, ua=ua, ub=ub, sbf=sbf: e.tensor_tensor(out=sbf[:, :], in0=ua[:, :], in1=ub[:, :], op=ALU.mult),
                      reads=[("u", ia), ("u", ib)], writes=[("sbf", si2)])
                P.dma("sp", self.s["zt"][fc, :, ts], sbf[:, :], reads=[("sbf", si2)], writes=[("zt", fc, tb)],
                      semkey="sbf%d" % si2)

    def phaseE2(self, l, last):
        P, d = self.P, self.d
        zbuf = self.Zb(0, NFC * 512).rearrange("p (k t) -> p k t", k=NFC)
        rbuf = self.Xv(0, 8192).rearrange("p (k t) -> p k t", k=16)
        xbt = self.Xv(8192, 4096).bitcast(BF16).rearrange("p (k t) -> p k t", k=16)
        orow = [self.Xv(8192, 2048), self.Xv(10240, 2048)]
        rkeys = [("rbuf", dc) for dc in range(16)]
        if not last:
            self.load_spw(l + 1)
        self.ld_zbuf(zbuf, 0)
        self.ld_rbuf(rbuf, rkeys, self.s["x1f"], slice(0, 512), [("x1f", 0, q4) for q4 in range(4)])
        for tb in range(4):
            ts = slice(tb * 512, (tb + 1) * 512)
            nts = slice((tb + 1) * 512, (tb + 2) * 512)
            nrd = [("x1f", tb + 1, q4) for q4 in range(4)]
            pend = []
            for dc in range(16):
                if tb == 0:
                    wv, wkey = self.wload(d["w_down"][l, :, dc * 128:(dc + 1) * 128], NFC, 128, "wdown")
                    si = wkey[1]
                    P.dma("sp", self.s["wdc"][dc], self.wb[si][:, 0:NFC * 128], reads=[wkey], writes=[("wdc", dc)],
                          semkey="wdcst%d" % si)
                else:
                    si = self.rot("wb", 2)
                    wkey = ("wb", si)
                    wv = self.wb[si][:, 0:NFC * 128].rearrange("p (k n) -> p k n", k=NFC)
                    P.dma("pool", self.wb[si][:, 0:NFC * 128], self.s["wdc"][dc], reads=[("wdc", dc)], writes=[wkey],
                          semkey="wb%d" % si)
                b = self.rot("ps8", 4)
                for kc in range(NFC):
                    self.mm(self.ps[b][:, :], ("ps", b), wv[:, kc, :], zbuf[:, kc, :], kc == 0, kc == NFC - 1,
                            [wkey, ("zbuf", kc // 11)])
                    if dc == 15 and tb < 3 and kc % 11 == 10:
                        self.ld_zbuf_chunk(zbuf, tb + 1, kc // 11)
                P.dve(lambda e, dc=dc, b=b: e.scalar_tensor_tensor(out=rbuf[:, dc, :], in0=rbuf[:, dc, :], scalar=ALPHA,
                                                                   in1=self.ps[b][:, :], op0=ALU.mult, op1=ALU.add),
                      reads=[("ps", b), ("rbuf", dc)], writes=[("rbuf", dc)])
                pend.append(self.ln_stats(rbuf, rkeys, dc, 512))
                if len(pend) > 1:
                    pend.pop(0)()
            while pend:
                pend.pop(0)()
            mean, rstd = self.ln_scale(512)
            if last:
                for dc in range(16):
                    self.ln_norm(rbuf, rkeys, dc, 448, 464, 512, mean, rstd)
                for c in range(4):
                    oi = self.rot("orow", 2)
                    orw = orow[oi]
                    for dq in range(4):
                        b = self.rot("ps8", 4)
                        for j in range(4):
                            dc = 4 * dq + j
                            self.tr(self.ps[b][:, j * 128:(j + 1) * 128], ("ps", b), rbuf[:, dc, c * 128:(c + 1) * 128], 128,
                                    [("rbuf", dc)])
                        if dq % 2 == 0:
                            P.act(lambda e, orw=orw, b=b, dq=dq: e.copy(out=orw[:, dq * 512:(dq + 1) * 512], in_=self.ps[b][:, :]),
                                  reads=[("ps", b)], writes=[("orow", oi, dq)])
                        else:
                            P.dve(lambda e, orw=orw, b=b, dq=dq: e.tensor_copy(out=orw[:, dq * 512:(dq + 1) * 512], in_=self.ps[b][:, :]),
                                  reads=[("ps", b)], writes=[("orow", oi, dq)])
                    t0 = tb * 512 + c * 128
                    o = P.dma("sp", d["out"][t0:t0 + 128, :], orw, reads=[("orow", oi, dq) for dq in range(4)],
                              writes=[("out", t0)], semkey="orow%d" % oi)
                    self.outs.append(o)
                if tb < 3:
                    self.ld_rbuf(rbuf, rkeys, self.s["x1f"], nts, nrd)
            else:
                for q4 in range(4):
                    for dc in range(4 * q4, 4 * q4 + 4):
                        self.ln_norm(rbuf, rkeys, dc, 448, 464, 512, mean, rstd, xbt=xbt)
                    P.dma("sp", self.s["xres"][4 * q4:4 * q4 + 4, :, ts].rearrange("k p t -> p k t"), rbuf[:, 4 * q4:4 * q4 + 4, :],
                          reads=rkeys[4 * q4:4 * q4 + 4], writes=[("xres", tb, q4)], semkey="rbufst%d" % q4)
                    P.dma("sp", self.s["xb"][4 * q4:4 * q4 + 4, :, ts].rearrange("k p t -> p k t"), xbt[:, 4 * q4:4 * q4 + 4, :],
                          reads=[("xbt", dc) for dc in range(4 * q4, 4 * q4 + 4)], writes=[("xb", tb, q4)], semkey="xbtst%d" % q4)
                self.special_proj(l + 1, rbuf, rkeys, tb * 512, 512)
                if tb < 3:
                    self.ld_rbuf(rbuf, rkeys, self.s["x1f"], nts, nrd)

    def ld_zbuf(self, zbuf, tb):
        for j in range(4):
            self.ld_zbuf_chunk(zbuf, tb, j)

    def ld_zbuf_chunk(self, zbuf, tb, j):
        ts = slice(tb * 512, (tb + 1) * 512)
        if True:
            self.P.dma("sp", zbuf[:, 11 * j:11 * j + 11, :], self.s["zt"][11 * j:11 * j + 11, :, ts].rearrange("k p t -> p k t"),
                       reads=[("zt", fc, tb) for fc in range(11 * j, 11 * j + 11)], writes=[("zbuf", j)], semkey="zbufld%d" % j)

    def load_params(self, l):
        d = self.d
        for j in range(3):
            self.load_cols(64 + 88 * j, d["conv_w"][l, j], 88)
        self.load_cols(328, d["conv_b"][l], 88)
        self.load_cols(416, d["ln1_g"][l], 16)
        self.load_cols(432, d["ln1_b"][l], 16)
        self.load_cols(448, d["ln2_g"][l], 16)
        self.load_cols(464, d["ln2_b"][l], 16)

    def build(self):
        P = self.P
        self.alloc()
        self.gbuf = [self.sb("gbuf%d" % i, [128, 1536], BF16) for i in range(2)]
        upto = getattr(self, "upto", None)
        order = ["setup", "A", "B", "S", "M", "mA", "mB", "mC", "D1", "D2", "E1", "E2"]

        self.marks = []

        def on(name):
            self.marks.append((name, len(P.ops["pe"])))
            return upto is None or order.index(name) <= order.index(upto)
        self.setup()
        P.barrier()
        for li in range(self.n_layers):
            l = self.first_layer + li
            last = li == self.n_layers - 1
            if li == 0:
                if on("A"):
                    self.phaseA(l)
            else:
                P.dma("sp", self.xT(), self.s["xb"].rearrange("k p t -> p k t"), reads=[("xb", tb, q) for tb in range(4) for q in range(4)],
                      writes=[("xT", i) for i in range(8)], semkey="xTld")
            if on("B"):
                self.load_params(l)
                self.phaseB(l)
                P.barrier()
            if on("S"):
                self.phaseS(l)
            if on("M"):
                self.phaseM(l)
                P.barrier()
            if on("mA"):
                self.mixerA(l)
                P.barrier()
            if on("mB"):
                self.mixerB(l)
                P.barrier()
            if on("mC"):
                self.mixerC(l)
                P.barrier()
            if on("D1"):
                self.phaseD1(l)
                P.barrier()
            if on("D2"):
                self.phaseD2(l)
                P.barrier()
            if on("E1"):
                self.phaseE1(l)
                P.barrier()
            if on("E2"):
                self.phaseE2(l, last)
                P.barrier()
        stats = P.emit(final_ops=self.outs)
        self.st.close()
        return self.nc, stats


_PROG_CACHE = {}


def _get_prog(key, **kw):
    if key not in _PROG_CACHE:
        b = Builder(**kw)
        nc, stats = b.build()
        _PROG_CACHE[key] = nc
    return _PROG_CACHE[key]


def _in_maps(x, weights):
    consts = host_consts()
    maps = []
    for b in range(x.shape[0]):
        m = {"x": np.ascontiguousarray(x[b], dtype=np.float32)}
        for k in WEIGHT_SHAPES:
            m[k] = weights[k]
        for k in CONST_SHAPES:
            m["c_" + k] = consts[k]
        maps.append(m)
    return maps


def kernel(**inputs):
    x = np.asarray(inputs["x"], dtype=np.float32)
    weights = {k: np.ascontiguousarray(np.asarray(inputs[k], dtype=np.float32)) for k in WEIGHT_SHAPES}
    nc = _get_prog("full", n_layers=DEPTH, first_layer=0)
    res = run_bass_kernel_spmd(nc, _in_maps(x, weights), core_ids=list(range(8)))
    return np.stack([r["out"] for r in res.results], axis=0).astype(np.float32)
```
